# Optimizing a Trainium2 kernel written in Bass

```python
import math
import jax
import jax.numpy as jnp
from jax import lax
import numpy as np

D_MODEL = 1024
BATCH = 8
SEQ = 4096
DEPTH = 4

GRID_W = 64
CTX_LEN = 256
N_MIXERS = 3
D_FF = 4 * D_MODEL
EPS = 1e-6
ROPE_THETA = 10000.0
Q_BLOCK = 128
NEG_INF = -1e30

A_HEADS = 16
A_KV_HEADS = 4
A_GROUP = A_HEADS // A_KV_HEADS
A_HEAD_DIM = D_MODEL // A_HEADS
WINDOW = 128
B_HEADS = 8
B_HEAD_DIM = D_MODEL // (2 * B_HEADS)
C_HEADS = 16
C_Q_LORA = 384
C_KV_LORA = 256
C_NOPE = 64
C_ROPE = 32
C_V = 64

kernel_name = "hybrid_interleaved_diffusion_trunk"


def rms_norm(x, g):
    x32 = x.astype(jnp.float32)
    y = x32 * lax.rsqrt(jnp.mean(x32 * x32, axis=-1, keepdims=True) + EPS)
    return (y * g.astype(jnp.float32)).astype(x.dtype)


def modulate(x, shift, scale):
    return x * (1.0 + scale) + shift


def rope_2d(x, rows, cols):
    d = x.shape[-1]
    da = d // 2
    inv = ROPE_THETA ** (-jnp.arange(0, da, 2, dtype=jnp.float32) / da)
    shape = (1, x.shape[1]) + (1,) * (x.ndim - 3) + (da // 2,)

    def rot(xa, pos):
        ang = pos.astype(jnp.float32)[:, None] * inv[None, :]
        cos = jnp.cos(ang).reshape(shape).astype(x.dtype)
        sin = jnp.sin(ang).reshape(shape).astype(x.dtype)
        x1, x2 = jnp.split(xa, 2, axis=-1)
        return jnp.concatenate([x1 * cos - x2 * sin, x2 * cos + x1 * sin], axis=-1)

    return jnp.concatenate([rot(x[..., :da], rows), rot(x[..., da:], cols)], axis=-1)


def softmax_with_sink(s, sink):
    if sink is None:
        return jax.nn.softmax(s, axis=-1)
    col = jnp.broadcast_to(sink.astype(jnp.float32)[None, :, :, None, None], s.shape[:-1] + (1,))
    return jax.nn.softmax(jnp.concatenate([s, col], axis=-1), axis=-1)[..., :-1]


def dense_attention(q, k, v, scale, sink=None):
    B, Sq, G, R, dq = q.shape
    dv = v.shape[-1]
    nb = Sq // Q_BLOCK
    qb = jnp.moveaxis(q.reshape(B, nb, Q_BLOCK, G, R, dq), 1, 0)

    def block(qi):
        s = jnp.einsum("bqgrd,bkgd->bgrqk", qi, k).astype(jnp.float32) * scale
        p = softmax_with_sink(s, sink).astype(v.dtype)
        return jnp.einsum("bgrqk,bkgv->bqgrv", p, v)

    out = lax.map(block, qb)
    return jnp.moveaxis(out, 0, 1).reshape(B, Sq, G, R, dv)


def window_attention(q, k, v, kc, vc, scale, sink):
    B, S, G, R, dq = q.shape
    dv = v.shape[-1]
    nb = S // Q_BLOCK
    span = Q_BLOCK + 2 * WINDOW
    pad = ((0, 0), (WINDOW, WINDOW), (0, 0), (0, 0))
    kp = jnp.pad(k, pad)
    vp = jnp.pad(v, pad)
    qb = jnp.moveaxis(q.reshape(B, nb, Q_BLOCK, G, R, dq), 1, 0)
    offs = jnp.arange(span) - WINDOW
    band = jnp.abs(jnp.arange(Q_BLOCK)[:, None] - offs[None, :]) <= WINDOW

    def block(args):
        i, qi = args
        start = i * Q_BLOCK
        ki = lax.dynamic_slice_in_dim(kp, start, span, axis=1)
        vi = lax.dynamic_slice_in_dim(vp, start, span, axis=1)
        kpos = start + offs
        mask = band & ((kpos >= 0) & (kpos < S))[None, :]
        s_loc = jnp.einsum("bqgrd,bkgd->bgrqk", qi, ki).astype(jnp.float32) * scale
        s_loc = jnp.where(mask, s_loc, NEG_INF)
        s_ctx = jnp.einsum("bqgrd,bcgd->bgrqc", qi, kc).astype(jnp.float32) * scale
        p = softmax_with_sink(jnp.concatenate([s_loc, s_ctx], axis=-1), sink).astype(v.dtype)
        return (jnp.einsum("bgrqk,bkgv->bqgrv", p[..., :span], vi)
                + jnp.einsum("bgrqc,bcgv->bqgrv", p[..., span:], vc))

    out = lax.map(block, (jnp.arange(nb), qb))
    return jnp.moveaxis(out, 0, 1).reshape(B, S, G, R, dv)


def mixer_window_gqa(u, uc, p, rows, cols, layer_idx, need_ctx):
    B, S, _ = u.shape
    C = uc.shape[1]
    nq = A_HEADS * A_HEAD_DIM
    nk = A_KV_HEADS * A_HEAD_DIM
    scale = A_HEAD_DIM ** -0.5
    sink = p["sink"].reshape(A_KV_HEADS, A_GROUP)
    q, k, v = jnp.split(u @ p["w_qkv"], [nq, nq + nk], axis=-1)
    q = rope_2d(q.reshape(B, S, A_KV_HEADS, A_GROUP, A_HEAD_DIM), rows, cols)
    k = rope_2d(k.reshape(B, S, A_KV_HEADS, A_HEAD_DIM), rows, cols)
    v = v.reshape(B, S, A_KV_HEADS, A_HEAD_DIM)
    kc, vc = jnp.split(uc @ p["w_qkv"][:, nq:], 2, axis=-1)
    kc = kc.reshape(B, C, A_KV_HEADS, A_HEAD_DIM)
    vc = vc.reshape(B, C, A_KV_HEADS, A_HEAD_DIM)
    y = window_attention(q, k, v, kc, vc, scale, sink).reshape(B, S, nq) @ p["w_o"]
    yc = None
    if need_ctx:
        qc = (uc @ p["w_qkv"][:, :nq]).reshape(B, C, A_KV_HEADS, A_GROUP, A_HEAD_DIM)
        yc = dense_attention(qc, kc, vc, scale, sink).reshape(B, C, nq) @ p["w_o"]
    return y, yc


def mixer_diff_attention(u, uc, p, rows, cols, layer_idx, need_ctx):
    B, S, _ = u.shape
    C = uc.shape[1]
    d = B_HEAD_DIM
    nqk = B_HEADS * 2 * d
    scale = d ** -0.5
    lam_init = 0.8 - 0.6 * math.exp(-0.3 * layer_idx)
    lam = p["lambda"].astype(jnp.float32)
    lam_full = jnp.exp(jnp.sum(lam[0] * lam[1])) - jnp.exp(jnp.sum(lam[2] * lam[3])) + lam_init
    q, k, v = jnp.split(u @ p["w_qkv"], [nqk, 2 * nqk], axis=-1)
    q = rope_2d(q.reshape(B, S, B_HEADS, 2, d), rows, cols)
    k = rope_2d(k.reshape(B, S, B_HEADS, 2, d), rows, cols)
    v = v.reshape(B, S, B_HEADS, 2 * d)
    kc, vc = jnp.split(uc @ p["w_qkv"][:, nqk:], 2, axis=-1)
    kc = kc.reshape(B, C, B_HEADS, 2, d)
    vc = vc.reshape(B, C, B_HEADS, 2 * d)

    def diff_attend(qq, kk, vv):
        a1 = dense_attention(qq[:, :, :, 0:1], kk[:, :, :, 0], vv, scale)
        a2 = dense_attention(qq[:, :, :, 1:2], kk[:, :, :, 1], vv, scale)
        o = a1[:, :, :, 0] - lam_full.astype(a1.dtype) * a2[:, :, :, 0]
        o = rms_norm(o, p["subln"]) * (1.0 - lam_init)
        return o.reshape(o.shape[0], o.shape[1], -1) @ p["w_o"]

    y = diff_attend(q, jnp.concatenate([k, kc], axis=1), jnp.concatenate([v, vc], axis=1))
    yc = None
    if need_ctx:
        qc = (uc @ p["w_qkv"][:, :nqk]).reshape(B, C, B_HEADS, 2, d)
        yc = diff_attend(qc, kc, vc)
    return y, yc


def mixer_mla(u, uc, p, rows, cols, layer_idx, need_ctx):
    B, S, _ = u.shape
    C = uc.shape[1]
    dqk = C_NOPE + C_ROPE
    scale = dqk ** -0.5

    def queries(cq):
        n = cq.shape[1]
        return (rms_norm(cq, p["q_norm"]) @ p["w_uq"]).reshape(B, n, C_HEADS, dqk)

    def keys_values(ckv, kr):
        n = ckv.shape[1]
        kv = (rms_norm(ckv, p["kv_norm"]) @ p["w_ukv"]).reshape(B, n, C_HEADS, C_NOPE + C_V)
        k = jnp.concatenate([kv[..., :C_NOPE], jnp.broadcast_to(kr, (B, n, C_HEADS, C_ROPE))], axis=-1)
        return k, kv[..., C_NOPE:]

    cq, ckv, kr = jnp.split(u @ p["w_in"], [C_Q_LORA, C_Q_LORA + C_KV_LORA], axis=-1)
    q = queries(cq)
    q = jnp.concatenate([q[..., :C_NOPE], rope_2d(q[..., C_NOPE:], rows, cols)], axis=-1)
    k, v = keys_values(ckv, rope_2d(kr[:, :, None, :], rows, cols))
    ckv_c, kr_c = jnp.split(uc @ p["w_in"][:, C_Q_LORA:], [C_KV_LORA], axis=-1)
    kc, vc = keys_values(ckv_c, kr_c[:, :, None, :])
    y = dense_attention(q[:, :, :, None], jnp.concatenate([k, kc], axis=1),
                        jnp.concatenate([v, vc], axis=1), scale)
    y = y.reshape(B, S, C_HEADS * C_V) @ p["w_o"]
    yc = None
    if need_ctx:
        qc = queries(uc @ p["w_in"][:, :C_Q_LORA])
        yc = dense_attention(qc[:, :, :, None], kc, vc, scale).reshape(B, C, C_HEADS * C_V) @ p["w_o"]
    return y, yc


def squared_relu_mlp(u, w1, w2):
    h = jax.nn.relu(u @ w1)
    return (h * h) @ w2


def setup_inputs(seed: int = 0) -> dict:
    key = jax.random.key(seed)
    keys = iter(jax.random.split(key, 128))

    def normal(shape, scale=1.0):
        return jax.random.normal(next(keys), shape, jnp.float32) * scale

    inp = {
        "x": normal((BATCH, SEQ, D_MODEL)),
        "c": normal((BATCH, D_MODEL)),
        "ctx": normal((BATCH, CTX_LEN, D_MODEL)),
        "c_ctx": normal((D_MODEL,)),
    }
    for i in range(DEPTH):
        kind = i % N_MIXERS
        inp[f"l{i}_ada_w"] = normal((D_MODEL, 6 * D_MODEL), 0.5 * D_MODEL ** -0.5)
        inp[f"l{i}_ada_b"] = normal((6 * D_MODEL,), 0.02)
        inp[f"l{i}_norms"] = 1.0 + normal((4, D_MODEL), 0.05)
        if kind == 0:
            inp[f"l{i}_w_qkv"] = normal((D_MODEL, (A_HEADS + 2 * A_KV_HEADS) * A_HEAD_DIM), D_MODEL ** -0.5)
            inp[f"l{i}_sink"] = normal((A_HEADS,), 0.5)
            inp[f"l{i}_w_o"] = normal((A_HEADS * A_HEAD_DIM, D_MODEL), (A_HEADS * A_HEAD_DIM) ** -0.5)
        elif kind == 1:
            inp[f"l{i}_w_qkv"] = normal((D_MODEL, 3 * B_HEADS * 2 * B_HEAD_DIM), D_MODEL ** -0.5)
            inp[f"l{i}_lambda"] = normal((4, B_HEAD_DIM), 0.1)
            inp[f"l{i}_subln"] = 1.0 + normal((2 * B_HEAD_DIM,), 0.05)
            inp[f"l{i}_w_o"] = normal((B_HEADS * 2 * B_HEAD_DIM, D_MODEL), (B_HEADS * 2 * B_HEAD_DIM) ** -0.5)
        else:
            inp[f"l{i}_w_in"] = normal((D_MODEL, C_Q_LORA + C_KV_LORA + C_ROPE), D_MODEL ** -0.5)
            inp[f"l{i}_q_norm"] = 1.0 + normal((C_Q_LORA,), 0.05)
            inp[f"l{i}_kv_norm"] = 1.0 + normal((C_KV_LORA,), 0.05)
            inp[f"l{i}_w_uq"] = normal((C_Q_LORA, C_HEADS * (C_NOPE + C_ROPE)), C_Q_LORA ** -0.5)
            inp[f"l{i}_w_ukv"] = normal((C_KV_LORA, C_HEADS * (C_NOPE + C_V)), C_KV_LORA ** -0.5)
            inp[f"l{i}_w_o"] = normal((C_HEADS * C_V, D_MODEL), (C_HEADS * C_V) ** -0.5)
        inp[f"l{i}_mlp_w1"] = normal((D_MODEL, D_FF), D_MODEL ** -0.5)
        inp[f"l{i}_mlp_w2"] = normal((D_FF, D_MODEL), D_FF ** -0.5)
    return inp


def reference(x, c, ctx, c_ctx,
              l0_ada_w, l0_ada_b, l0_norms, l0_w_qkv, l0_sink, l0_w_o, l0_mlp_w1, l0_mlp_w2,
              l1_ada_w, l1_ada_b, l1_norms, l1_w_qkv, l1_lambda, l1_subln, l1_w_o, l1_mlp_w1, l1_mlp_w2,
              l2_ada_w, l2_ada_b, l2_norms, l2_w_in, l2_q_norm, l2_kv_norm, l2_w_uq, l2_w_ukv, l2_w_o,
              l2_mlp_w1, l2_mlp_w2,
              l3_ada_w, l3_ada_b, l3_norms, l3_w_qkv, l3_sink, l3_w_o, l3_mlp_w1, l3_mlp_w2):
    B, S, D = x.shape
    ROWS = S // GRID_W
    rows = jnp.repeat(jnp.arange(ROWS, dtype=jnp.int32), GRID_W)
    cols = jnp.tile(jnp.arange(GRID_W, dtype=jnp.int32), ROWS)

    layers = [
        dict(ada_w=l0_ada_w, ada_b=l0_ada_b, norms=l0_norms, w_qkv=l0_w_qkv, sink=l0_sink, w_o=l0_w_o,
             mlp_w1=l0_mlp_w1, mlp_w2=l0_mlp_w2),
        dict(ada_w=l1_ada_w, ada_b=l1_ada_b, norms=l1_norms, w_qkv=l1_w_qkv, subln=l1_subln, w_o=l1_w_o,
             mlp_w1=l1_mlp_w1, mlp_w2=l1_mlp_w2, **{"lambda": l1_lambda}),
        dict(ada_w=l2_ada_w, ada_b=l2_ada_b, norms=l2_norms, w_in=l2_w_in, q_norm=l2_q_norm,
             kv_norm=l2_kv_norm, w_uq=l2_w_uq, w_ukv=l2_w_ukv, w_o=l2_w_o,
             mlp_w1=l2_mlp_w1, mlp_w2=l2_mlp_w2),
        dict(ada_w=l3_ada_w, ada_b=l3_ada_b, norms=l3_norms, w_qkv=l3_w_qkv, sink=l3_sink, w_o=l3_w_o,
             mlp_w1=l3_mlp_w1, mlp_w2=l3_mlp_w2),
    ]
    mixers = (mixer_window_gqa, mixer_diff_attention, mixer_mla)

    sc = jax.nn.silu(c)
    scc = jax.nn.silu(c_ctx)
    h, hc = x, ctx
    for i in range(DEPTH):
        p = layers[i]
        last = i == DEPTH - 1
        g = p["norms"]
        mod = jnp.split((sc @ p["ada_w"] + p["ada_b"])[:, None, :], 6, axis=-1)
        n_mod_c = 2 if last else 6
        mod_c = jnp.split(scc @ p["ada_w"][:, :n_mod_c * D] + p["ada_b"][:n_mod_c * D], n_mod_c, axis=-1)

        u = modulate(rms_norm(h, g[0]), mod[0], mod[1])
        uc = modulate(rms_norm(hc, g[0]), mod_c[0], mod_c[1])
        y, yc = mixers[i % N_MIXERS](u, uc, p, rows, cols, i, not last)
        h = h + mod[2] * rms_norm(y, g[1])
        u = modulate(rms_norm(h, g[2]), mod[3], mod[4])
        h = h + mod[5] * rms_norm(squared_relu_mlp(u, p["mlp_w1"], p["mlp_w2"]), g[3])
        if not last:
            hc = hc + mod_c[2] * rms_norm(yc, g[1])
            uc = modulate(rms_norm(hc, g[2]), mod_c[3], mod_c[4])
            hc = hc + mod_c[5] * rms_norm(squared_relu_mlp(uc, p["mlp_w1"], p["mlp_w2"]), g[3])
    return h
```

```python
import math
import os
from contextlib import ExitStack

import numpy as np
import concourse.bass as bass
import concourse.mybir as mybir
from concourse.bass_utils import run_bass_kernel_spmd

F32 = mybir.dt.float32
BF16 = mybir.dt.bfloat16
AF = mybir.ActivationFunctionType
ALU = mybir.AluOpType

D = 1024
S = 4096
CL = 256
ST = S + CL
DFF = 4096
EPS = 1e-6
NCORES = 8


class Res:
    __slots__ = ("name", "w", "r", "sem", "cnt", "excl")

    def __init__(self, name, sem=None):
        self.name = name
        self.excl = False
        self.w = {}
        self.r = {}
        self.sem = sem
        self.cnt = 0


class Eng:
    def __init__(self, name, kind, sem, self_sync):
        self.name = name
        self.kind = kind
        self.sem = sem
        self.self_sync = self_sync
        self.base = 0
        self.nops = 0
        self.q = []
        self.confirmed = {}
        self.waited = set()
        self.rank = {}
        self.tot_ins = 0


class DSem:
    __slots__ = ("sem", "cnt")

    def __init__(self, sem):
        self.sem = sem
        self.cnt = 0


class _Rec:
    def __getattr__(self, name):
        return lambda *a, **k: (name, a, k)


_REC = _Rec()


class Ctx:
    def __init__(self, nc, stack, n_dma_sems=40):
        self.nc = nc
        self.stack = stack
        self.engs = {}
        self.all_res = []
        self.dma_pool = []
        for i in range(n_dma_sems):
            r = Res("dma%d" % i)
            r.sem = {q: DSem(stack.enter_context(nc.semaphore("dma%d%s" % (i, q)))) for q in ("sp", "pool")}
            self.dma_pool.append(r)
        self.dma_used = 0

    def add_engine(self, name, kind, self_sync=True):
        e = Eng(name, kind, self.stack.enter_context(self.nc.semaphore("e_" + name)), self_sync)
        self.engs[name] = e
        return e

    def res(self, name):
        r = Res(name)
        self.all_res.append(r)
        return r

    def slot(self, name):
        r = self.dma_pool[self.dma_used]
        self.dma_used += 1
        r.name = name
        return r

    def _deps(self, e, reads, writes):
        need = {}
        for r in reads:
            for k, ev in r.w.items():
                if k not in need or need[k][2] < ev[2]:
                    need[k] = ev
            if r.excl:
                for k, ev in r.r.items():
                    if k != e.name and (k not in need or need[k][2] < ev[2]):
                        need[k] = ev
        for w in writes:
            for d in (w.w, w.r):
                for k, ev in d.items():
                    if k not in need or need[k][2] < ev[2]:
                        need[k] = ev
        for k, ev in need.items():
            typ, obj, val = ev
            if typ == "e" and obj is e and not e.self_sync:
                continue
            if e.confirmed.get(k, 0) < val:
                e.q.append(("wait", typ, obj, val))
                e.confirmed[k] = val
                if typ == "e":
                    obj.waited.add(val)

    @staticmethod
    def _post(key, ev, reads, writes):
        for r in reads:
            r.r[key] = ev
        for w in writes:
            w.w = {key: ev}
            w.r = {}

    def op(self, eng, reads, writes, fn):
        e = self.engs[eng]
        self._deps(e, reads, writes)
        e.nops += 1
        fn = fn(_REC)
        e.q.append(("op", fn, e.nops))
        self._post(e.name, ("e", e, e.nops), reads, writes)

    def dma(self, eng, slot, reads, writes, out, in_):
        e = self.engs[eng]
        self._deps(e, reads, writes)
        ds = slot.sem["pool" if eng == "pool" else "sp"]
        ds.cnt += 16
        e.q.append(("dma", out, in_, ds))
        self._post(id(ds), ("d", ds, ds.cnt), reads, writes)

    def end_phase(self):
        for e in self.engs.values():
            for o in self.engs.values():
                if o is e or o.nops == 0:
                    continue
                if e.confirmed.get(o.name, 0) < o.nops:
                    e.q.append(("wait", "e", o, o.nops))
                    e.confirmed[o.name] = o.nops
                    o.waited.add(o.nops)
            for sl in self.dma_pool[:self.dma_used]:
                for s in sl.sem.values():
                    if s.cnt and e.confirmed.get(id(s), 0) < s.cnt:
                        e.q.append(("wait", "d", s, s.cnt))
                        e.confirmed[id(s)] = s.cnt
        for e in self.engs.values():
            srt = sorted(e.waited)
            e.rank = {idx: e.base + i + 1 for i, idx in enumerate(srt)}
        with self.nc.Block() as block:
            for e in self.engs.values():
                def body(h, e=e):
                    for it in e.q:
                        if it[0] == "wait":
                            _, typ, obj, val = it
                            if typ == "e":
                                h.wait_ge(obj.sem, obj.rank[val])
                            else:
                                h.wait_ge(obj.sem, val)
                        elif it[0] == "op":
                            name, a, k = it[1]
                            ins = getattr(h, name)(*a, **k)
                            if it[2] in e.waited:
                                ins.then_inc(e.sem, 1)
                        else:
                            h.dma_start(out=it[1], in_=it[2]).then_inc(it[3].sem, 16)
                getattr(block, e.kind)(body)
        for e in self.engs.values():
            e.tot_ins += len(e.q)
            e.base += len(e.waited)
            e.nops = 0
            e.q = []
            e.waited = set()
            e.rank = {}
            e.confirmed = {k: v for k, v in e.confirmed.items() if not isinstance(k, str)}
        for r in self.all_res:
            r.w = {}
            r.r = {}
        self.all_res = []
        for s in self.dma_pool:
            s.w = {}
            s.r = {}
        self.dma_used = 0


def _rope_tables(d, nrep, pad_lo):
    da = d // 2
    inv = 10000.0 ** (-np.arange(0, da, 2, dtype=np.float32) / da)
    t = np.arange(S)
    rows = (t // 64).astype(np.float32)
    cols = (t % 64).astype(np.float32)
    q = da // 2
    C = np.zeros((d, S), np.float32)
    Sg = np.zeros((d, S), np.float32)
    partner = np.zeros(d, np.int64)
    for half, pos in enumerate((rows, cols)):
        ang = (pos[None, :].astype(np.float32) * inv[:, None].astype(np.float32)).astype(np.float32)
        c = np.cos(ang).astype(np.float32)
        s = np.sin(ang).astype(np.float32)
        b = half * da
        C[b:b + q] = c
        C[b + q:b + da] = c
        Sg[b:b + q] = -s
        Sg[b + q:b + da] = s
        partner[b:b + q] = np.arange(b + q, b + da)
        partner[b + q:b + da] = np.arange(b, b + q)
    n = pad_lo + nrep * d
    Cf = np.ones((n, S), np.float32)
    Sf = np.zeros((n, S), np.float32)
    P = np.zeros((n, n), np.float32)
    for r in range(nrep):
        o = pad_lo + r * d
        Cf[o:o + d] = C
        Sf[o:o + d] = Sg
        for m in range(d):
            P[o + partner[m], o + m] = 1.0
    return Cf, Sf, P


def _consts():
    c64, s64, p64 = _rope_tables(64, 2, 0)
    c96, s96, p96 = _rope_tables(32, 1, 64)
    kk = np.arange(128)[:, None]
    qq = np.arange(128)[None, :]
    mask = np.concatenate([(qq >= kk), np.ones((128, 128), bool), (qq <= kk)], axis=1).astype(np.float32)
    return {
        "k_ident": np.eye(128, dtype=np.float32),
        "k_c64": c64, "k_s64": s64, "k_p64": p64,
        "k_c96": c96, "k_s96": s96, "k_p96": p96,
        "k_mask": mask,
    }


LAYER_W = {
    0: [("w_qkv", [D, 1536]), ("sink", [128, 16]), ("w_o", [D, D])],
    1: [("w_qkv", [D, 3072]), ("lam", [128, 256]), ("subln", [128, 1]), ("w_o", [D, D])],
    2: [("w_in", [D, 672]), ("qn", [128, 3]), ("kvn", [128, 2]), ("w_uq", [384, 1536]), ("w_ukv", [256, 2048]),
        ("w_o", [D, D])],
}


def build(n_layers=4, dbg=False, stop_after=None):
    nc = bass.Bass("TRN2", target_bir_lowering=False)
    I = {}

    def din(name, shape, dt=F32):
        I[name] = nc.dram_tensor(name, list(shape), dt, kind="ExternalInput").ap()
        return I[name]

    din("x", [S, D]); din("ctx", [CL, D]); din("c2", [128, 8, 2])
    for k, v in _consts().items():
        din(k, v.shape)
    for L in range(n_layers):
        din(f"l{L}_ada_w", [D, 6 * D]); din(f"l{L}_ada_b", [128, 48]); din(f"l{L}_norms", [128, 4, 8])
        for nm, shp in LAYER_W[L % 3]:
            din(f"l{L}_{nm}", shp)
        din(f"l{L}_mlp_w1", [8, 128, 8 * 512]); din(f"l{L}_mlp_w2", [8, 128, 32 * 128])
    out = nc.dram_tensor("out", [S, D], F32, kind="ExternalOutput").ap()
    hT = nc.dram_tensor("hT", [D, ST], F32).ap()
    qS = nc.dram_tensor("qS", [16 * 128, ST], BF16).ap()
    kS = nc.dram_tensor("kS", [16 * 128, ST], BF16).ap()
    krS = nc.dram_tensor("krS", [32, ST], BF16).ap()
    vS = nc.dram_tensor("vS", [ST, 1024], BF16).ap()
    oS = nc.dram_tensor("oS", [D, ST], BF16).ap()
    dbg_out = {}
    if dbg:
        for nm, src in (("d_hT", hT), ("d_oS", oS), ("d_qS", qS), ("d_kS", kS), ("d_vS", vS), ("d_krS", krS)):
            dbg_out[nm] = (nc.dram_tensor(nm, list(src.shape), src.dtype, kind="ExternalOutput").ap(), src)
        dbg_mv = nc.dram_tensor("d_mv", [128, 96], F32, kind="ExternalOutput").ap()

    TILES = [(t0, 512, 0) for t0 in range(0, S, 512)] + [(S, CL, 1)]

    with ExitStack() as gs:
        cx = Ctx(nc, gs)
        for n, k in (("sp", "sync"), ("pe", "tensor"), ("act", "scalar"), ("dve", "vector"), ("pool", "gpsimd")):
            cx.add_engine(n, k, self_sync=(n != "pe"))

        uid = [0]

        def sbuf(st, name, shape, dt):
            uid[0] += 1
            return st.enter_context(nc.sbuf_tensor("%s_%d" % (name, uid[0]), list(shape), dt))

        ones_bf = sbuf(gs, "ones_bf", [128, 128], BF16)
        ident = sbuf(gs, "ident", [128, 128], F32)
        eps_t = sbuf(gs, "eps_t", [128, 1], F32)
        sc2 = sbuf(gs, "sc2", [128, 8, 2], BF16)
        MVall = sbuf(gs, "MV", [128, 4, 6, 2, 8], F32)
        PS = [gs.enter_context(nc.psum_tensor("ps%d" % i, [128, 512], F32)) for i in range(8)]

        def psres():
            rr = [cx.res("ps%d" % i) for i in range(8)]
            for r_ in rr:
                r_.excl = True
            return rr

        def load_w(slot, dst, src, nk, eng="pool"):
            for k in range(nk):
                cx.dma(eng, slot, [], [slot], dst[:, k, :], src[k * 128:(k + 1) * 128, :])

        def m_chunks(Ls, st, bank, r_ps):
            if not Ls:
                return []
            wa = [sbuf(st, "wa%d" % i, [128, 8, 1024], BF16) for i in range(2)]
            s_wa = [cx.slot("wa%d" % i) for i in range(2)]
            adab = sbuf(st, "adab", [128, 48], F32); s_adab = cx.slot("adab")
            gn = sbuf(st, "gn", [128, 4, 8], F32); s_gn = cx.slot("gn")
            mod = sbuf(st, "mod", [128, 48, 2], F32)
            out = []
            for L in Ls:
                out += m_chunks1(L, bank, r_ps, wa, s_wa, adab, s_adab, gn, s_gn, mod)
            return out

        def m_chunks1(L, bank, r_ps, wa, s_wa, adab, s_adab, gn, s_gn, mod):
            W = lambda nm: I[f"l{L}_{nm}"]
            MVl = MVall[:, L]
            r_mv = cx.res("mv%d" % L); r_t = cx.res("mt%d" % L)
            chunks = []

            def ld(m):
                load_w(s_wa[m % 2], wa[m % 2], W("ada_w")[:, m * 1024:(m + 1) * 1024], 8)

            def first():
                cx.dma("sp", s_adab, [], [s_adab], adab[:], W("ada_b")[:, :])
                cx.dma("sp", s_gn, [], [s_gn], gn[:], W("norms")[:, :, :])
                ld(0)
                ld(1)
            chunks.append(first)

            def blk(m):
                b = m % 2
                for jb in range(8):
                    j = m * 8 + jb
                    for dc in range(8):
                        cx.op("pe", [s_wa[b]], [r_ps[bank]],
                              lambda h, b=b, jb=jb, j=j, dc=dc: h.matmul(
                                  PS[bank][:, 2 * j:2 * j + 2], lhsT=wa[b][:, dc, jb * 128:(jb + 1) * 128],
                                  rhs=sc2[:, dc, 0:2], start=(dc == 0), stop=(dc == 7), skip_group_check=True))
                if m + 2 < 6:
                    ld(m + 2)
            for m in range(6):
                chunks.append(lambda m=m: blk(m))

            def fin():
                cx.op("dve", [r_ps[bank], s_adab], [r_t],
                      lambda h: h.tensor_tensor(out=mod[:, :, :], in0=PS[bank][:, 0:96].rearrange("p (j s) -> p j s", s=2),
                                                in1=adab[:, :].unsqueeze(2).to_broadcast([128, 48, 2]), op=ALU.add))
                for s_ in range(2):
                    for (dst, scale_j, g_i) in ((0, 8, 0), (3, 32, 2)):
                        cx.op("dve", [r_t, s_gn], [r_mv],
                              lambda h, s_=s_, dst=dst, scale_j=scale_j, g_i=g_i: h.scalar_tensor_tensor(
                                  out=MVl[:, dst, s_, :], in0=mod[:, scale_j:scale_j + 8, s_], scalar=1.0,
                                  in1=gn[:, g_i, :], op0=ALU.add, op1=ALU.mult))
                    for (dst, shift_j) in ((1, 0), (4, 24)):
                        cx.op("dve", [r_t], [r_mv],
                              lambda h, s_=s_, dst=dst, shift_j=shift_j: h.tensor_copy(
                                  out=MVl[:, dst, s_, :], in_=mod[:, shift_j:shift_j + 8, s_]))
                    for (dst, gate_j, g_i) in ((2, 16, 1), (5, 40, 3)):
                        cx.op("dve", [r_t, s_gn], [r_mv],
                              lambda h, s_=s_, dst=dst, gate_j=gate_j, g_i=g_i: h.tensor_tensor(
                                  out=MVl[:, dst, s_, :], in0=mod[:, gate_j:gate_j + 8, s_], in1=gn[:, g_i, :],
                                  op=ALU.mult))
            chunks.append(fin)
            return chunks


        with ExitStack() as st:
            r_c = cx.res("c")
            r_ps = psres()
            c2f = sbuf(st, "c2f", [128, 8, 2], F32)
            s_c2 = cx.slot("c2f"); s_id = cx.slot("ident")
            cx.dma("sp", s_c2, [], [s_c2], c2f[:], I["c2"][:, :, :])
            cx.dma("sp", s_id, [], [s_id], ident[:], I["k_ident"][:, :])
            cx.op("dve", [], [r_c], lambda h: h.memset(ones_bf[:], 1.0))
            cx.op("dve", [], [r_c], lambda h: h.memset(eps_t[:], EPS))
            cx.op("act", [s_c2], [r_c], lambda h: h.activation(out=sc2[:], in_=c2f[:], func=AF.Silu))
            xs = [sbuf(st, "xs%d" % i, [128, 4, D], F32) for i in range(2)]
            s_xs = [cx.slot("xs%d" % i) for i in range(2)]
            hts = [sbuf(st, "xh%d" % i, [128, 8, 512], F32) for i in range(2)]
            s_hts = [cx.slot("xh%d" % i) for i in range(2)]
            bg0 = None
            for ti, (t0, T, strm) in enumerate(TILES):
                if bg0 is None:
                    bg0 = m_chunks([0], st, 7, r_ps)
                if bg0:
                    bg0.pop(0)()
                b = ti % 2
                ns = T // 128
                src = I["x"][t0:t0 + T, :] if strm == 0 else I["ctx"][:, :]
                cx.dma("sp", s_xs[b], [], [s_xs[b]], xs[b][:, 0:ns, :], src.rearrange("(s p) d -> p s d", p=128))
                for c in range(8):
                    pb = c % 4
                    for s_ in range(ns):
                        cx.op("pe", [s_xs[b], s_id], [r_ps[pb]],
                              lambda h, pb=pb, s_=s_, c=c, b=b: h.transpose(
                                  out=PS[pb][:, s_ * 128:(s_ + 1) * 128], in_=xs[b][:, s_, c * 128:(c + 1) * 128],
                                  identity=ident[:]))
                    if c % 2 == 0:
                        cx.op("act", [r_ps[pb]], [s_hts[b]],
                              lambda h, pb=pb, c=c, b=b, T=T: h.copy(out=hts[b][:, c, 0:T], in_=PS[pb][:, 0:T]))
                    else:
                        cx.op("dve", [r_ps[pb]], [s_hts[b]],
                              lambda h, pb=pb, c=c, b=b, T=T: h.tensor_copy(out=hts[b][:, c, 0:T], in_=PS[pb][:, 0:T]))
                cx.dma("pool", s_hts[b], [s_hts[b]], [],
                       hT[:, t0:t0 + T].rearrange("(c p) t -> p c t", p=128), hts[b][:, :, 0:T])
            while bg0:
                bg0.pop(0)()
            cx.end_phase()

        def rstd_from_psum(ps_i, r_ps, rstd, r_rstd, T, nfeat):
            cx.op("act", [r_ps[ps_i]], [r_rstd],
                  lambda h: h.activation(out=rstd[:, 0:T], in_=PS[ps_i][:, 0:T], func=AF.Ln, bias=eps_t[:, 0:1],
                                         scale=1.0 / nfeat))
            cx.op("act", [r_rstd], [r_rstd],
                  lambda h: h.activation(out=rstd[:, 0:T], in_=rstd[:, 0:T], func=AF.Exp, scale=-0.5))

        def norm_front(st_res, ht, T, A, Bv, u, r_u, sq, r_sq, un, r_un, rstd, r_rstd, ps_i, r_ps):
            cx.op("act", [st_res], [r_sq], lambda h: h.activation(out=sq[:, :, 0:T], in_=ht[:, :, 0:T], func=AF.Square))
            for c in range(8):
                cx.op("pe", [r_sq], [r_ps[ps_i]],
                      lambda h, c=c: h.matmul(PS[ps_i][:, 0:T], lhsT=ones_bf[:, :], rhs=sq[:, c, 0:T],
                                              start=(c == 0), stop=(c == 7)))
            rstd_from_psum(ps_i, r_ps, rstd, r_rstd, T, D)
            cx.op("dve", [st_res, r_rstd], [r_un],
                  lambda h: h.tensor_tensor(out=un[:, :, 0:T], in0=ht[:, :, 0:T],
                                            in1=rstd[:, 0:T].unsqueeze(1).to_broadcast([128, 8, T]), op=ALU.mult))
            for c in range(8):
                if c % 2 == 0:
                    cx.op("act", [r_un], [r_u],
                          lambda h, c=c: h.activation(out=u[:, c, 0:T], in_=un[:, c, 0:T], func=AF.Identity,
                                                      bias=Bv[:, c:c + 1], scale=A[:, c:c + 1]))
                else:
                    cx.op("dve", [r_un], [r_u],
                          lambda h, c=c: h.tensor_scalar(out=u[:, c, 0:T], in0=un[:, c, 0:T], scalar1=A[:, c:c + 1],
                                                         scalar2=Bv[:, c:c + 1], op0=ALU.mult, op1=ALU.add))

        def resid_back(y, r_y, ht, r_ht, T, G, sq, r_sq, rstd, r_rstd, ps_i, r_ps):
            cx.op("act", [r_y], [r_sq], lambda h: h.activation(out=sq[:, :, 0:T], in_=y[:, :, 0:T], func=AF.Square))
            for c in range(8):
                cx.op("pe", [r_sq], [r_ps[ps_i]],
                      lambda h, c=c: h.matmul(PS[ps_i][:, 0:T], lhsT=ones_bf[:, :], rhs=sq[:, c, 0:T],
                                              start=(c == 0), stop=(c == 7)))
            rstd_from_psum(ps_i, r_ps, rstd, r_rstd, T, D)
            for c in range(8):
                cx.op("dve", [r_y, r_rstd], [r_y],
                      lambda h, c=c: h.tensor_tensor(out=y[:, c, 0:T], in0=y[:, c, 0:T], in1=rstd[:, 0:T], op=ALU.mult))
                cx.op("dve", [r_y, r_ht], [r_ht],
                      lambda h, c=c: h.scalar_tensor_tensor(out=ht[:, c, 0:T], in0=y[:, c, 0:T], scalar=G[:, c:c + 1],
                                                            in1=ht[:, c, 0:T], op0=ALU.mult, op1=ALU.add))

        for L in range(n_layers):
            kind = L % 3
            last = (L == n_layers - 1)
            W = lambda nm: I[f"l{L}_{nm}"]
            MV = MVall[:, L]
            phase_A(nc, cx, I, L, kind, last, TILES, sbuf, psres, PS, MV, ones_bf, eps_t, norm_front, load_w,
                    hT, qS, kS, krS, vS)
            if stop_after == ("A", L):
                break
            bgL = [x for x in {0: [1, 2], 2: [3]}.get(L, []) if x < n_layers]
            phase_B(nc, cx, I, L, kind, last, sbuf, psres, PS, ones_bf, eps_t, qS, kS, krS, vS, oS,
                    bg_make=(lambda st_, r_ps_, bgL=bgL: m_chunks(bgL, st_, 6, r_ps_)), bg_every=(9 if kind == 0 else 4))
            if stop_after == ("B", L):
                break
            with ExitStack() as st:
                r_ps = psres()
                wo = sbuf(st, "wo", [128, 8, D], BF16); s_wo = cx.slot("wo")
                load_w(s_wo, wo, W("w_o"), 8)
                ot = [sbuf(st, "ot%d" % i, [128, 8, 512], BF16) for i in range(3)]
                s_ot = [cx.slot("ot%d" % i) for i in range(3)]
                ht = [sbuf(st, "cht%d" % i, [128, 8, 512], F32) for i in range(3)]
                s_ht = [cx.slot("cht%d" % i) for i in range(3)]
                y = [sbuf(st, "cy%d" % i, [128, 8, 512], F32) for i in range(2)]; r_y = [cx.res("y%d" % i) for i in range(2)]
                sq = [sbuf(st, "csq%d" % i, [128, 8, 512], BF16) for i in range(2)]; r_sq = [cx.res("sq%d" % i) for i in range(2)]
                rstd = [sbuf(st, "crstd%d" % i, [128, 512], F32) for i in range(2)]; r_rstd = [cx.res("rstd%d" % i) for i in range(2)]
                tiles = TILES[:-1] if last else TILES
                cpend = []

                def c_tail(ti):
                    t0, T, strm = tiles[ti]
                    b = ti % 2
                    b3 = ti % 3
                    G = MV[:, 2, strm, :]
                    for c in range(8):
                        cx.op("pe", [r_sq[b]], [r_ps[0]],
                              lambda h, c=c: h.matmul(PS[0][:, 0:T], lhsT=ones_bf[:, :], rhs=sq[b][:, c, 0:T],
                                                      start=(c == 0), stop=(c == 7)))
                    rstd_from_psum(0, r_ps, rstd[b], r_rstd[b], T, D)
                    for c in range(8):
                        cx.op("dve", [r_y[b], r_rstd[b]], [r_y[b]],
                              lambda h, c=c: h.tensor_tensor(out=y[b][:, c, 0:T], in0=y[b][:, c, 0:T], in1=rstd[b][:, 0:T],
                                                             op=ALU.mult))
                        cx.op("dve", [r_y[b], s_ht[b3]], [s_ht[b3]],
                              lambda h, c=c: h.scalar_tensor_tensor(out=ht[b3][:, c, 0:T], in0=y[b][:, c, 0:T],
                                                                    scalar=G[:, c:c + 1], in1=ht[b3][:, c, 0:T],
                                                                    op0=ALU.mult, op1=ALU.add))
                    cx.dma("pool", s_ht[b3], [s_ht[b3]], [],
                           hT[:, t0:t0 + T].rearrange("(c p) t -> p c t", p=128), ht[b3][:, :, 0:T])

                for ti, (t0, T, strm) in enumerate(tiles):
                    b = ti % 2
                    b3 = ti % 3
                    cx.dma("sp", s_ot[b3], [], [s_ot[b3]], ot[b3][:, :, 0:T],
                           oS[:, t0:t0 + T].rearrange("(c p) t -> p c t", p=128))
                    cx.dma("sp", s_ht[b3], [], [s_ht[b3]], ht[b3][:, :, 0:T],
                           hT[:, t0:t0 + T].rearrange("(c p) t -> p c t", p=128))
                    for c in range(8):
                        pb = 1 + (c % 4)
                        for k in range(8):
                            cx.op("pe", [s_wo, s_ot[b3]], [r_ps[pb]],
                                  lambda h, pb=pb, k=k, c=c, b3=b3, T=T: h.matmul(
                                      PS[pb][:, 0:T], lhsT=wo[:, k, c * 128:(c + 1) * 128], rhs=ot[b3][:, k, 0:T],
                                      start=(k == 0), stop=(k == 7)))
                        cx.op("act", [r_ps[pb]], [r_y[b]],
                              lambda h, pb=pb, c=c, T=T, b=b: h.copy(out=y[b][:, c, 0:T], in_=PS[pb][:, 0:T]))
                    cx.op("act", [r_y[b]], [r_sq[b]],
                          lambda h, b=b, T=T: h.activation(out=sq[b][:, :, 0:T], in_=y[b][:, :, 0:T], func=AF.Square))
                    while cpend:
                        cpend.pop(0)()
                    cpend.append(lambda ti=ti: c_tail(ti))
                while cpend:
                    cpend.pop(0)()
                cx.end_phase()
            if stop_after == ("C", L):
                break

            with ExitStack() as st:
                r_ps = psres()
                TD = 1024
                dtiles = [(t0, TD, 0) for t0 in range(0, S, TD)] + ([] if last else [(S, CL, 1)])
                ht = sbuf(st, "dht", [128, 8, TD], F32); s_hth = [cx.slot("dht0"), cx.slot("dht1")]
                u = sbuf(st, "du", [128, 8, TD], BF16); r_uh = [cx.res("u0"), cx.res("u1")]
                sq = sbuf(st, "dsq", [128, 8, 512], BF16); r_sq = cx.res("sq")
                y = sbuf(st, "dy", [128, 8, TD], F32); r_y = [cx.res("y0"), cx.res("y1")]
                rstd = sbuf(st, "drstd", [128, 512], F32); r_rstd = cx.res("rstd")
                hid = sbuf(st, "hid", [128, 32, TD], BF16); r_hid = [cx.res("hid%d" % j) for j in range(32)]
                rl = [sbuf(st, "rl%d" % i, [128, 512], F32) for i in range(2)]; r_rl = [cx.res("rl%d" % i) for i in range(2)]
                w1 = [sbuf(st, "w1_%d" % i, [128, 8, 512], BF16) for i in range(3)]
                s_w1 = [cx.slot("w1_%d" % i) for i in range(3)]
                w2 = [sbuf(st, "w2_%d" % i, [128, 32, 128], BF16) for i in range(2)]
                s_w2 = [cx.slot("w2_%d" % i) for i in range(2)]
                nw1 = 0
                nw2 = 0
                nev = 0
                def halves_of(T):
                    return [(o, min(512, T - o)) for o in range(0, T, 512)]

                def load_half(ti, hi):
                    t0_, T_, _ = dtiles[ti]
                    o_, Th_ = halves_of(T_)[hi]
                    cx.dma("sp", s_hth[hi], [], [s_hth[hi]], ht[:, :, o_:o_ + Th_],
                           hT[:, t0_ + o_:t0_ + o_ + Th_].rearrange("(c p) t -> p c t", p=128))

                for hi in range(len(halves_of(dtiles[0][1]))):
                    load_half(0, hi)
                for ti, (t0, T, strm) in enumerate(dtiles):
                    halves = halves_of(T)
                    for hi, (o, Th) in enumerate(halves):
                        norm_front(s_hth[hi], ht[:, :, o:o + Th], Th, MV[:, 3, strm, :], MV[:, 4, strm, :],
                                   u[:, :, o:o + Th], r_uh[hi], sq, r_sq, y[:, :, o:o + Th], r_y[hi], rstd, r_rstd, 0, r_ps)
                    for jp in range(4):
                        bs = []
                        for jg in (2 * jp, 2 * jp + 1):
                            b = nw1 % 3
                            nw1 += 1
                            bs.append((jg, b))
                            cx.dma("pool", s_w1[b], [], [s_w1[b]], w1[b][:, :, :].rearrange("p k f -> p (k f)"),
                                   W("mlp_w1")[jg, :, :])
                        for hi, (o, Th) in enumerate(halves):
                            for (jg, b) in bs:
                                for jj in range(4):
                                    j = jg * 4 + jj
                                    pb = 1 + (nev % 4)
                                    for k in range(8):
                                        cx.op("pe", [s_w1[b], r_uh[hi]], [r_ps[pb]],
                                              lambda h, pb=pb, b=b, k=k, jj=jj, o=o, Th=Th: h.matmul(
                                                  PS[pb][:, 0:Th], lhsT=w1[b][:, k, jj * 128:(jj + 1) * 128],
                                                  rhs=u[:, k, o:o + Th], start=(k == 0), stop=(k == 7)))
                                    rb = nev % 2
                                    nev += 1
                                    cx.op("act", [r_ps[pb]], [r_rl[rb]],
                                          lambda h, pb=pb, rb=rb, Th=Th: h.activation(out=rl[rb][:, 0:Th],
                                                                                      in_=PS[pb][:, 0:Th], func=AF.Relu))
                                    cx.op("dve", [r_rl[rb]], [r_hid[j]],
                                          lambda h, rb=rb, j=j, o=o, Th=Th: h.tensor_tensor(
                                              out=hid[:, j, o:o + Th], in0=rl[rb][:, 0:Th], in1=rl[rb][:, 0:Th],
                                              op=ALU.mult))
                    for c in range(8):
                        b = nw2 % 2
                        nw2 += 1
                        cx.dma("pool", s_w2[b], [], [s_w2[b]], w2[b][:, :, :].rearrange("p j f -> p (j f)"),
                               W("mlp_w2")[c, :, :])
                        for hi, (o, Th) in enumerate(halves):
                            pb = 5 + ((2 * c + hi) % 3)
                            for j in range(32):
                                cx.op("pe", [s_w2[b], r_hid[j]], [r_ps[pb]],
                                      lambda h, pb=pb, b=b, j=j, o=o, Th=Th: h.matmul(
                                          PS[pb][:, 0:Th], lhsT=w2[b][:, j, :], rhs=hid[:, j, o:o + Th],
                                          start=(j == 0), stop=(j == 31)))
                            if (c + hi) % 2 == 0:
                                cx.op("act", [r_ps[pb]], [r_y[hi]],
                                      lambda h, pb=pb, c=c, o=o, Th=Th: h.copy(out=y[:, c, o:o + Th], in_=PS[pb][:, 0:Th]))
                            else:
                                cx.op("dve", [r_ps[pb]], [r_y[hi]],
                                      lambda h, pb=pb, c=c, o=o, Th=Th: h.tensor_copy(out=y[:, c, o:o + Th], in_=PS[pb][:, 0:Th]))
                    for hi, (o, Th) in enumerate(halves):
                        resid_back(y[:, :, o:o + Th], r_y[hi], ht[:, :, o:o + Th], s_hth[hi], Th, MV[:, 5, strm, :], sq, r_sq,
                                   rstd, r_rstd, 0, r_ps)
                        cx.dma("sp", s_hth[hi], [s_hth[hi]], [],
                               hT[:, t0 + o:t0 + o + Th].rearrange("(c p) t -> p c t", p=128), ht[:, :, o:o + Th])
                        if ti + 1 < len(dtiles) and hi < len(halves_of(dtiles[ti + 1][1])):
                            load_half(ti + 1, hi)
                cx.end_phase()
            if stop_after == ("D", L):
                break

        with ExitStack() as st:
            r_ps = psres()
            hts = [sbuf(st, "yh%d" % i, [128, 8, 512], F32) for i in range(2)]
            s_hts = [cx.slot("yh%d" % i) for i in range(2)]
            os_ = [sbuf(st, "yo%d" % i, [128, 4, D], F32) for i in range(2)]
            s_os = [cx.slot("yo%d" % i) for i in range(2)]
            s_id = cx.slot("ident2")
            for ti, (t0, T, strm) in enumerate(TILES[:-1]):
                b = ti % 2
                cx.dma("sp", s_hts[b], [], [s_hts[b]], hts[b][:, :, :], hT[:, t0:t0 + T].rearrange("(c p) t -> p c t", p=128))
                for s_ in range(4):
                    for half in range(2):
                        pb = (s_ * 2 + half) % 4
                        for cc in range(4):
                            c = half * 4 + cc
                            cx.op("pe", [s_hts[b]], [r_ps[pb]],
                                  lambda h, pb=pb, cc=cc, c=c, s_=s_, b=b: h.transpose(
                                      out=PS[pb][:, cc * 128:(cc + 1) * 128], in_=hts[b][:, c, s_ * 128:(s_ + 1) * 128],
                                      identity=ident[:]))
                        if half == 0:
                            cx.op("act", [r_ps[pb]], [s_os[b]],
                                  lambda h, pb=pb, s_=s_, b=b: h.copy(out=os_[b][:, s_, 0:512], in_=PS[pb][:, :]))
                        else:
                            cx.op("dve", [r_ps[pb]], [s_os[b]],
                                  lambda h, pb=pb, s_=s_, b=b: h.tensor_copy(out=os_[b][:, s_, 512:1024], in_=PS[pb][:, :]))
                cx.dma("pool", s_os[b], [s_os[b]], [], out[t0:t0 + T, :].rearrange("(s p) d -> p s d", p=128), os_[b][:, :, :])
            if dbg:
                s_dd = cx.slot("dbgcopy")
                for nm, (dst, src) in dbg_out.items():
                    nr = src.shape[0]
                    for i in range(0, nr, 256):
                        cx.dma("sp", s_dd, [], [], dst[i:min(nr, i + 256)], src[i:min(nr, i + 256)])
            cx.end_phase()
        stats = {n: e.tot_ins for n, e in cx.engs.items()}
        stats["sem_base"] = {n: e.base for n, e in cx.engs.items()}
        stats["dma_max"] = max(d_.cnt for s in cx.dma_pool for d_ in s.sem.values())
    return nc, stats


def phase_A(nc, cx, I, L, kind, last, TILES, sbuf, psres, PS, MV, ones_bf, eps_t, norm_front, load_w,
            hT, qS, kS, krS, vS):
    W = lambda nm: I[f"l{L}_{nm}"]
    with ExitStack() as st:
        r_ps = psres()
        ht = [sbuf(st, "aht%d" % i, [128, 8, 512], F32) for i in range(2)]
        s_ht = [cx.slot("aht%d" % i) for i in range(2)]
        u = [sbuf(st, "au%d" % i, [128, 8, 512], BF16) for i in range(2)]
        r_u = [cx.res("au%d" % i) for i in range(2)]
        sq = sbuf(st, "asq", [128, 8, 512], BF16); r_sq = cx.res("sq")
        un = sbuf(st, "aun", [128, 8, 512], F32); r_un = cx.res("un")
        rstd = sbuf(st, "arstd", [128, 512], F32); r_rstd = cx.res("rstd")
        NR = 128 if kind < 2 else 96
        Ct = sbuf(st, "Ct", [NR, S], F32); St_ = sbuf(st, "St", [NR, S], F32); s_tab = cx.slot("tab")
        perm = sbuf(st, "perm", [NR, NR], BF16); s_perm = cx.slot("perm")
        sfx = "64" if kind < 2 else "96"
        cx.dma("pool", s_perm, [], [s_perm], perm[:, :], I["k_p" + sfx][:, :])
        ob = [sbuf(st, "aob%d" % i, [128, 512], BF16) for i in range(4)]
        s_ob = [cx.slot("aob%d" % i) for i in range(4)]
        xb = [sbuf(st, "axb%d" % i, [128, 512], BF16) for i in range(2)]
        r_xb = [cx.res("axb%d" % i) for i in range(2)]
        t1 = [sbuf(st, "at1_%d" % i, [128, 512], F32) for i in range(2)]
        r_t1 = [cx.res("at1_%d" % i) for i in range(2)]
        t2 = [sbuf(st, "at2_%d" % i, [128, 512], F32) for i in range(2)]
        r_t2 = [cx.res("at2_%d" % i) for i in range(2)]
        vb = [sbuf(st, "avb%d" % i, [128, 1024], BF16) for i in range(2)]
        s_vb = [cx.slot("avb%d" % i) for i in range(2)]
        cnt = {"p": 0, "ob": 0, "v": 0, "vb": 0, "ev": 0}
        pend = []

        def emit_fm(nk, lhsT_fn, rhs_fn, M, T, rope, t0, deps, dst, r0=0, r1=None):
            r1 = M if r1 is None else r1
            i = cnt["p"]; cnt["p"] += 1
            pb = (1, 2, 7)[i % 3]
            for k in range(nk):
                cx.op("pe", deps, [r_ps[pb]],
                      lambda h, k=k: h.matmul(PS[pb][0:M, 0:T], lhsT=lhsT_fn(k), rhs=rhs_fn(k), start=(k == 0),
                                              stop=(k == nk - 1)))
            flush_pend()
            pend.append(lambda: fm_tail(i, pb, M, T, rope, t0, dst, r0, r1))

        def flush_pend():
            while pend:
                pend.pop(0)()

        def fm_tail(i, pb, M, T, rope, t0, dst, r0, r1):
            oi = cnt["ob"] % 4; cnt["ob"] += 1
            _fc = os.environ.get("FM_CUT", "")
            if _fc == "mm":
                return
            if not rope:
                if cnt["ev"] % 2 == 0:
                    cx.op("act", [r_ps[pb]], [s_ob[oi]], lambda h: h.copy(out=ob[oi][0:M, 0:T], in_=PS[pb][0:M, 0:T]))
                else:
                    cx.op("dve", [r_ps[pb]], [s_ob[oi]],
                          lambda h: h.tensor_copy(out=ob[oi][0:M, 0:T], in_=PS[pb][0:M, 0:T]))
                cnt["ev"] += 1
            else:
                xi = i % 2
                pb2 = 3 + (i % 2)
                cx.op("act", [r_ps[pb]], [r_xb[xi]], lambda h: h.copy(out=xb[xi][0:M, 0:T], in_=PS[pb][0:M, 0:T]))
                cx.op("pe", [r_xb[xi], s_perm], [r_ps[pb2]],
                      lambda h: h.matmul(PS[pb2][0:M, 0:T], lhsT=perm[0:M, 0:M], rhs=xb[xi][0:M, 0:T], start=True,
                                         stop=True))
                _rc = os.environ.get("ROPE_CUT", "")
                if _rc == "mm":
                    return
                cx.op("dve", [r_ps[pb], s_tab, r_xb[xi]], [r_t1[xi]],
                      lambda h: h.tensor_tensor(out=t1[xi][0:M, 0:T], in0=PS[pb][0:M, 0:T], in1=Ct[0:M, t0:t0 + T],
                                                op=ALU.mult))
                if _rc == "t1":
                    return
                cx.op("dve", [r_ps[pb2], s_tab], [r_t2[xi]],
                      lambda h: h.tensor_tensor(out=t2[xi][0:M, 0:T], in0=PS[pb2][0:M, 0:T], in1=St_[0:M, t0:t0 + T],
                                                op=ALU.mult))
                if _rc == "t2":
                    return
                cx.op(os.environ.get("ROPE_ADD", "pool"), [r_t1[xi], r_t2[xi]], [s_ob[oi]],
                      lambda h: h.tensor_tensor(out=ob[oi][0:M, 0:T], in0=t1[xi][0:M, 0:T], in1=t2[xi][0:M, 0:T],
                                                op=ALU.add))
            if _fc == "ev":
                return
            if _fc == "vs":
                dst = vS[t0:t0 + 128, 0:T]
            if _fc == "pool":
                cx.dma("pool", s_ob[oi], [s_ob[oi]], [], dst, ob[oi][r0:r1, 0:T])
                return
            cx.dma("sp", s_ob[oi], [s_ob[oi]], [], dst, ob[oi][r0:r1, 0:T])

        def emit_v(nk, lhsT_fn, rhs_fn, ncol, groups, T, t0, deps):
            for s_ in range(T // 128):
                bi = cnt["vb"] % 2; cnt["vb"] += 1
                for g in range(groups):
                    pb = 5 + (cnt["v"] % 2); cnt["v"] += 1
                    for k in range(nk):
                        cx.op("pe", deps, [r_ps[pb]],
                              lambda h, k=k, pb=pb, g=g: h.matmul(PS[pb][:, 0:ncol], lhsT=lhsT_fn(k, s_),
                                                                  rhs=rhs_fn(k, g), start=(k == 0), stop=(k == nk - 1)))
                    flush_pend()
                    if g % 2 == 0:
                        cx.op("act", [r_ps[pb]], [s_vb[bi]],
                              lambda h, pb=pb, g=g: h.copy(out=vb[bi][:, g * ncol:(g + 1) * ncol], in_=PS[pb][:, 0:ncol]))
                    else:
                        cx.op("dve", [r_ps[pb]], [s_vb[bi]],
                              lambda h, pb=pb, g=g: h.tensor_copy(out=vb[bi][:, g * ncol:(g + 1) * ncol],
                                                                  in_=PS[pb][:, 0:ncol]))
                nv = ncol * groups
                cx.dma("sp", s_vb[bi], [s_vb[bi]], [], vS[t0 + s_ * 128:t0 + (s_ + 1) * 128, 0:nv], vb[bi][:, 0:nv])

        if kind == 0:
            Wt = sbuf(st, "awt", [128, 8, 1792], BF16)
            s_wb = [cx.slot("awt%d" % i) for i in range(4)]
            wslot = lambda col: s_wb[min(col // 512, 3)] if col < 1536 else s_wb[3]
            for cb in range(2):
                load_w(s_wb[cb], Wt[:, :, cb * 512:(cb + 1) * 512], W("w_qkv")[:, cb * 512:(cb + 1) * 512], 8)
            for g in range(4):
                for hf in range(2):
                    cx.dma("pool", s_wb[2], [], [s_wb[2]], Wt[:, :, 1024 + g * 128 + hf * 64:1024 + g * 128 + hf * 64 + 64],
                           W("w_qkv")[:, 1024 + g * 64:1024 + (g + 1) * 64].rearrange("(k p) f -> p k f", p=128))
            cx.dma("pool", s_wb[3], [], [s_wb[3]], Wt[:, :, 1536:1792],
                   W("w_qkv")[:, 1280:1536].rearrange("(k p) f -> p k f", p=128))
            s_w = s_wb[3]
            NQ, KOFF, NKC, VOFF, NVC, VG = 8, 1024, 4, 1536, 256, 1
        elif kind == 1:
            Wt = sbuf(st, "awt", [128, 8, 3072], BF16)
            s_wb = [cx.slot("awt%d" % i) for i in range(6)]
            wslot = lambda col: s_wb[col // 512]
            for cb in range(6):
                load_w(s_wb[cb], Wt[:, :, cb * 512:(cb + 1) * 512], W("w_qkv")[:, cb * 512:(cb + 1) * 512], 8)
            s_w = s_wb[5]
            NQ, KOFF, NKC, VOFF, NVC, VG = 8, 1024, 8, 2048, 512, 2
        else:
            Wt = sbuf(st, "awt", [128, 8, 672], BF16); s_w = cx.slot("awt")
            load_w(s_w, Wt, W("w_in"), 8)
            Wq = sbuf(st, "awq", [128, 3, 1536], BF16); s_wq = cx.slot("awq")
            load_w(s_wq, Wq, W("w_uq"), 3)
            Wkv = sbuf(st, "awkv", [128, 2, 2, 1024], BF16); s_wkv = cx.slot("awkv")
            for k in range(2):
                for a in range(2):
                    cx.dma("pool", s_wkv, [], [s_wkv], Wkv[:, k, a, :].rearrange("p (h e) -> p h e", e=64),
                           W("w_ukv")[k * 128:(k + 1) * 128, :].rearrange("p (h e) -> p h e", e=128)[:, :, a * 64:(a + 1) * 64])
            qn = sbuf(st, "aqn", [128, 3], F32); kvn = sbuf(st, "akvn", [128, 2], F32); s_nrm = cx.slot("anrm")
            cx.dma("sp", s_nrm, [], [s_nrm], qn[:, :], W("qn")[:, :])
            cx.dma("sp", s_nrm, [], [s_nrm], kvn[:, :], W("kvn")[:, :])
            cqf = sbuf(st, "cqf", [128, 5, 512], F32); r_cqf = cx.res("cqf")
            cqn = sbuf(st, "cqn", [128, 5, 512], BF16); r_cqn = cx.res("cqn")
            rs2 = sbuf(st, "ars2", [128, 512], F32); r_rs2 = cx.res("rs2")

        def load_ht(ti):
            t0, T, strm = TILES[ti]
            b = ti % 2
            cx.dma("sp", s_ht[b], [], [s_ht[b]], ht[b][:, :, 0:T], hT[:, t0:t0 + T].rearrange("(c p) t -> p c t", p=128))

        import os
        _cut = os.environ.get("A_CUT", "")
        if _cut:
            TILES = TILES[:int(_cut[0])]
        def do_norm(ti):
            t0_, T_, strm_ = TILES[ti]
            b_ = ti % 2
            norm_front(s_ht[b_], ht[b_], T_, MV[:, 0, strm_, :], MV[:, 1, strm_, :], u[b_], r_u[b_], sq, r_sq, un, r_un,
                       rstd, r_rstd, 0, r_ps)

        load_ht(0)
        if len(TILES) > 1:
            load_ht(1)
        cx.dma("sp", s_tab, [], [s_tab], Ct[:, :], I["k_c" + sfx][:, :])
        cx.dma("sp", s_tab, [], [s_tab], St_[:, :], I["k_s" + sfx][:, :])
        do_norm(0)
        for ti, (t0, T, strm) in enumerate(TILES):
            b = ti % 2
            if ti + 1 < len(TILES):
                do_norm(ti + 1)
            if ti + 2 < len(TILES):
                load_ht(ti + 2)
            lat = (strm == 0) and not os.environ.get("NO_ROPE")
            need_q = lat or (not last)
            ub = u[b]
            if _cut and "n" in _cut:
                continue
            if kind < 2:
                if need_q and "q" not in _cut:
                    for c in range(NQ):
                        emit_fm(8, lambda k, c=c: Wt[:, k, c * 128:(c + 1) * 128], lambda k: ub[:, k, 0:T], 128, T, lat,
                                t0, [wslot(c * 128), r_u[b]], qS[c * 128:(c + 1) * 128, t0:t0 + T])
                for c in range(NKC if "k" not in _cut else 0):
                    _ko = 0 if os.environ.get("K_W") else KOFF
                    _kd = qS if os.environ.get("K_D") else kS
                    emit_fm(8, lambda k, c=c: Wt[:, k, _ko + c * 128:_ko + (c + 1) * 128], lambda k: ub[:, k, 0:T], 128,
                            T, lat, t0, [wslot(_ko + c * 128), r_u[b]], _kd[c * 128:(c + 1) * 128, t0:t0 + T])
                if "v" not in _cut:
                  emit_v(8, lambda k, s_: ub[:, k, s_ * 128:(s_ + 1) * 128],
                       lambda k, g: Wt[:, k, VOFF + g * NVC:VOFF + (g + 1) * NVC], NVC, VG, T, t0,
                       [wslot(VOFF), wslot(VOFF + NVC * VG - 1), r_u[b]])
            else:
                for c in range(5):
                    i = cnt["p"]; cnt["p"] += 1
                    pb = (1, 2, 7)[i % 3]
                    for k in range(8):
                        cx.op("pe", [s_w, r_u[b]], [r_ps[pb]],
                              lambda h, k=k, c=c, pb=pb: h.matmul(PS[pb][:, 0:T], lhsT=Wt[:, k, c * 128:(c + 1) * 128],
                                                                  rhs=ub[:, k, 0:T], start=(k == 0), stop=(k == 7)))
                    if c % 2 == 0:
                        cx.op("act", [r_ps[pb]], [r_cqf], lambda h, c=c, pb=pb: h.copy(out=cqf[:, c, 0:T], in_=PS[pb][:, 0:T]))
                    else:
                        cx.op("dve", [r_ps[pb]], [r_cqf],
                              lambda h, c=c, pb=pb: h.tensor_copy(out=cqf[:, c, 0:T], in_=PS[pb][:, 0:T]))
                emit_fm(8, lambda k: Wt[:, k, 576:672], lambda k: ub[:, k, 0:T], 96, T, lat, t0, [s_w, r_u[b]],
                        krS[:, t0:t0 + T], r0=64, r1=96)
                flush_pend()
                for (c0, ncn, nfeat, gv) in ((0, 3, 384, qn), (3, 2, 256, kvn)):
                    cx.op("act", [r_cqf], [r_sq],
                          lambda h, c0=c0, ncn=ncn: h.activation(out=sq[:, 0:ncn, 0:T], in_=cqf[:, c0:c0 + ncn, 0:T],
                                                                 func=AF.Square))
                    for c in range(ncn):
                        cx.op("pe", [r_sq], [r_ps[0]],
                              lambda h, c=c, ncn=ncn: h.matmul(PS[0][:, 0:T], lhsT=ones_bf[:, :], rhs=sq[:, c, 0:T],
                                                               start=(c == 0), stop=(c == ncn - 1)))
                    cx.op("act", [r_ps[0]], [r_rs2],
                          lambda h, nfeat=nfeat: h.activation(out=rs2[:, 0:T], in_=PS[0][:, 0:T], func=AF.Ln,
                                                              bias=eps_t[:, 0:1], scale=1.0 / nfeat))
                    cx.op("act", [r_rs2], [r_rs2],
                          lambda h: h.activation(out=rs2[:, 0:T], in_=rs2[:, 0:T], func=AF.Exp, scale=-0.5))
                    for c in range(ncn):
                        cx.op("dve", [r_cqf, r_rs2, s_nrm], [r_cqn],
                              lambda h, c=c, c0=c0, gv=gv: h.scalar_tensor_tensor(
                                  out=cqn[:, c0 + c, 0:T], in0=cqf[:, c0 + c, 0:T], scalar=gv[:, c:c + 1],
                                  in1=rs2[:, 0:T], op0=ALU.mult, op1=ALU.mult))
                if need_q:
                    for hq in range(16):
                        emit_fm(3, lambda k, hq=hq: Wq[:, k, hq * 96:(hq + 1) * 96], lambda k: cqn[:, k, 0:T], 96, T, lat,
                                t0, [s_wq, r_cqn], qS[hq * 128:hq * 128 + 96, t0:t0 + T])
                flush_pend()
                for hp in range(8):
                    i = cnt["p"]; cnt["p"] += 1
                    pb = (1, 2, 7)[i % 3]
                    for k in range(2):
                        cx.op("pe", [s_wkv, r_cqn], [r_ps[pb]],
                              lambda h, k=k, hp=hp, pb=pb: h.matmul(
                                  PS[pb][:, 0:T],
                                  lhsT=Wkv[:, k, 0, hp * 128:(hp + 1) * 128],
                                  rhs=cqn[:, 3 + k, 0:T], start=(k == 0), stop=(k == 1)))
                    oi = cnt["ob"] % 4; cnt["ob"] += 1
                    cx.op("act", [r_ps[pb]], [s_ob[oi]], lambda h, pb=pb, oi=oi: h.copy(out=ob[oi][:, 0:T], in_=PS[pb][:, 0:T]))
                    for a in range(2):
                        cx.dma("sp", s_ob[oi], [s_ob[oi]], [], kS[(2 * hp + a) * 128:(2 * hp + a) * 128 + 64, t0:t0 + T],
                               ob[oi][a * 64:(a + 1) * 64, 0:T])
                emit_v(2, lambda k, s_: cqn[:, 3 + k, s_ * 128:(s_ + 1) * 128],
                       lambda k, g: Wkv[:, k, 1, g * 512:(g + 1) * 512],
                       512, 2, T, t0, [s_wkv, r_cqn])
            flush_pend()
        flush_pend()
        cx.end_phase()


def phase_B(nc, cx, I, L, kind, last, sbuf, psres, PS, ones_bf, eps_t, qS, kS, krS, vS, oS, bg_make=None, bg_every=9):
    W = lambda nm: I[f"l{L}_{nm}"]
    NKT = ST // 128
    with ExitStack() as st:
        r_ps = psres()
        bg = bg_make(st, r_ps) if (bg_make is not None and kind != 1) else []
        uctr = [0]

        def tick():
            uctr[0] += 1
            if bg and uctr[0] % bg_every == 0:
                bg.pop(0)()
        r_c = cx.res("bconst")
        Q = [sbuf(st, "bq%d" % i, [128, ST], BF16) for i in range(2)]; s_Q = [cx.slot("bq%d" % i) for i in range(2)]
        K = [sbuf(st, "bk%d" % i, [128, ST], BF16) for i in range(2)]; s_K = [cx.slot("bk%d" % i) for i in range(2)]
        V = [sbuf(st, "bv%d" % i, [128, NKT, 128], BF16) for i in range(2)]; s_V = [cx.slot("bv%d" % i) for i in range(2)]
        NP = 4
        P = [sbuf(st, "bp%d" % i, [128, 512], BF16) for i in range(NP)]; r_P = [cx.res("bp%d" % i) for i in range(NP)]
        tmp = [sbuf(st, "bt%d" % i, [128, 512], F32) for i in range(6)]; r_tmp = [cx.res("bt%d" % i) for i in range(6)]
        obf = [sbuf(st, "bo%d" % i, [128, 512], BF16) for i in range(2)]; s_obf = [cx.slot("bo%d" % i) for i in range(2)]
        sqb = sbuf(st, "bsq", [128, 512], BF16); r_sqb = cx.res("bsq")
        ring = {"s": 0, "p": 0, "o": 0, "e": 0}
        pending = []
        pending_mid = []
        if kind == 0:
            mask = sbuf(st, "bmask", [128, 384], BF16); s_mask = cx.slot("bmask")
            cx.dma("pool", s_mask, [], [s_mask], mask[:, :], I["k_mask"][:, :])
            sink = sbuf(st, "bsink", [128, 16], F32); s_sink = cx.slot("bsink")
            cx.dma("sp", s_sink, [], [s_sink], sink[:, :], W("sink")[:, :])
            esink = sbuf(st, "besink", [128, 16], F32)
            cx.op("act", [s_sink], [r_c], lambda h: h.activation(out=esink[:, :], in_=sink[:, :], func=AF.Exp))
        if kind == 1:
            lam_init = 0.8 - 0.6 * math.exp(-0.3 * L)
            lam = sbuf(st, "blam", [128, 256], F32); s_lam = cx.slot("blam")
            subl = sbuf(st, "bsubl", [128, 1], F32)
            cx.dma("sp", s_lam, [], [s_lam], lam[:, :], W("lam")[:, :])
            cx.dma("sp", s_lam, [], [s_lam], subl[:, :], W("subln")[:, :])
            lt = sbuf(st, "blt", [128, 128], F32); ls = sbuf(st, "bls", [128, 4], F32)
            nlam = sbuf(st, "bnlam", [128, 1], F32); subs = sbuf(st, "bsubs", [128, 1], F32)
            r_l = cx.res("lamtmp")
            cx.op("dve", [s_lam], [r_l], lambda h: h.tensor_tensor(out=lt[:, 0:64], in0=lam[:, 0:64], in1=lam[:, 64:128], op=ALU.mult))
            cx.op("dve", [s_lam], [r_l], lambda h: h.tensor_tensor(out=lt[:, 64:128], in0=lam[:, 128:192], in1=lam[:, 192:256], op=ALU.mult))
            cx.op("dve", [r_l], [r_l], lambda h: h.reduce_sum(out=ls[:, 0:1], in_=lt[:, 0:64], axis=mybir.AxisListType.X))
            cx.op("dve", [r_l], [r_l], lambda h: h.reduce_sum(out=ls[:, 1:2], in_=lt[:, 64:128], axis=mybir.AxisListType.X))
            cx.op("act", [r_l], [r_l], lambda h: h.activation(out=ls[:, 2:4], in_=ls[:, 0:2], func=AF.Exp))
            cx.op("dve", [r_l], [r_c], lambda h: h.scalar_tensor_tensor(out=nlam[:, :], in0=ls[:, 3:4], scalar=-lam_init,
                                                                        in1=ls[:, 2:3], op0=ALU.add, op1=ALU.subtract))
            cx.op("dve", [s_lam], [r_c], lambda h: h.tensor_scalar(out=subs[:, :], in0=subl[:, :], scalar1=1.0 - lam_init,
                                                                   scalar2=None, op0=ALU.mult))
        if kind != 1:
            for i in range(2):
                cx.op("dve", [], [s_V[i]], lambda h, i=i: h.memset(V[i][:, :, 64:128], 1.0))

        def run_unit(steps, epilogue, gsize=1):
            groups = [steps[i:i + gsize] for i in range(0, len(steps), gsize)]
            ng = len(groups)
            nring = 4 // gsize if gsize > 1 else 4
            GLA = max(1, nring - 1) if gsize == 1 else 1
            GLA = int(os.environ.get("B_GLA", GLA))

            def emit_qk(g):
                for s in groups[g]:
                    sbk = ring["s"] % 4; ring["s"] += 1
                    s["sb"] = sbk
                    N = s["N"]
                    cx.op("pe", s["rd"], [r_ps[sbk]],
                          lambda h, s=s, sbk=sbk, N=N: h.matmul(PS[sbk][:, 0:N], lhsT=s["k"], rhs=s["q"], start=True,
                                                                stop=True))

            for g in range(min(GLA, ng)):
                emit_qk(g)
            for g in range(ng):
                if g + GLA < ng:
                    emit_qk(g + GLA)
                if g == min(2, ng - 1):
                    while pending_mid:
                        pending_mid.pop(0)()
                for s in groups[g]:
                    N = s["N"]
                    pi = ring["p"] % NP; ring["p"] += 1
                    s["pi"] = pi
                    sbk = s["sb"]
                    cx.op("act", [r_ps[sbk]], [r_P[pi]],
                          lambda h, s=s, N=N, pi=pi, sbk=sbk: h.activation(out=P[pi][:, 0:N], in_=PS[sbk][:, 0:N],
                                                                           func=AF.Exp, scale=s["scale"]))
                    if s.get("mask") is not None:
                        cx.op("dve", [r_P[pi], s_mask], [r_P[pi]],
                              lambda h, s=s, N=N, pi=pi: h.tensor_tensor(out=P[pi][:, 0:N], in0=P[pi][:, 0:N],
                                                                         in1=s["mask"], op=ALU.mult))
                for s in groups[g]:
                    N = s["N"]
                    pi = s["pi"]
                    for (obk, c0, lhsT, start, stop, rd) in s["pv"]:
                        cx.op("pe", [r_P[pi]] + rd, [r_ps[obk]],
                              lambda h, obk=obk, c0=c0, lhsT=lhsT, start=start, stop=stop, N=N, pi=pi: h.matmul(
                                  PS[obk][:, c0:c0 + N], lhsT=lhsT, rhs=P[pi][:, 0:N], start=start, stop=stop,
                                  skip_group_check=True))
            while pending:
                pending.pop(0)()
            epilogue()

        def epi_aug(obk, N, sink_col, dst):
            e = ring["e"] % 2; ring["e"] += 1
            epi_aug_body(obk, N, sink_col, dst, e)

        def epi_aug_body(obk, N, sink_col, dst, e):
            ta, tb = tmp[2 * e], tmp[2 * e + 1]
            ra, rb = r_tmp[2 * e], r_tmp[2 * e + 1]
            if sink_col is not None:
                cx.op("act", [r_ps[obk], r_c], [ra],
                      lambda h: h.activation(out=ta[64:128, 0:N], in_=PS[obk][64:128, 0:N], func=AF.Ln,
                                             bias=esink[64:128, sink_col:sink_col + 1], scale=1.0))
                cx.op("act", [ra], [rb],
                      lambda h: h.activation(out=tb[0:64, 0:N], in_=ta[64:128, 0:N], func=AF.Exp, scale=-1.0))
            else:
                cx.op("dve", [r_ps[obk]], [ra], lambda h: h.reciprocal(out=ta[64:128, 0:N], in_=PS[obk][64:128, 0:N]))
                cx.op("dve", [ra], [rb], lambda h: h.tensor_copy(out=tb[0:64, 0:N], in_=ta[64:128, 0:N]))
            cx.op("dve", [r_ps[obk], rb], [s_obf[e]],
                  lambda h: h.tensor_tensor(out=obf[e][0:64, 0:N], in0=PS[obk][0:64, 0:N], in1=tb[0:64, 0:N], op=ALU.mult))
            cx.dma("sp", s_obf[e], [s_obf[e]], [], dst, obf[e][0:64, 0:N])

        def epi_diff(N, dst):
            e = ring["e"] % 2; ring["e"] += 1
            t = tmp; r = r_tmp
            cx.op("dve", [r_ps[6]], [r[0]], lambda h: h.tensor_copy(out=t[0][:, 0:N], in_=PS[6][:, 0:N]))
            cx.op("dve", [r_ps[7]], [r[1]], lambda h: h.tensor_copy(out=t[1][:, 0:N], in_=PS[7][:, 0:N]))
            cx.op("dve", [r_ps[4]], [r[2]], lambda h: h.tensor_copy(out=t[2][:, 0:N], in_=PS[4][:, 0:N]))
            cx.op("dve", [r_ps[5]], [r[3]], lambda h: h.tensor_copy(out=t[3][:, 0:N], in_=PS[5][:, 0:N]))
            cx.op("dve", [r[0]], [r[0]], lambda h: h.reciprocal(out=t[0][:, 0:N], in_=t[0][:, 0:N]))
            cx.op("dve", [r[1]], [r[1]], lambda h: h.reciprocal(out=t[1][:, 0:N], in_=t[1][:, 0:N]))
            cx.op("dve", [r[2], r[0]], [r[2]],
                  lambda h: h.tensor_tensor(out=t[2][:, 0:N], in0=t[2][:, 0:N], in1=t[0][:, 0:N], op=ALU.mult))
            cx.op("dve", [r[3], r[1]], [r[3]],
                  lambda h: h.tensor_tensor(out=t[3][:, 0:N], in0=t[3][:, 0:N], in1=t[1][:, 0:N], op=ALU.mult))
            cx.op("dve", [r[2], r[3], r_c], [r[4]],
                  lambda h: h.scalar_tensor_tensor(out=t[4][:, 0:N], in0=t[3][:, 0:N], scalar=nlam[:, 0:1],
                                                   in1=t[2][:, 0:N], op0=ALU.mult, op1=ALU.add))
            cx.op("pool", [r[4]], [r_sqb],
                  lambda h: h.tensor_tensor(out=sqb[:, 0:N], in0=t[4][:, 0:N], in1=t[4][:, 0:N], op=ALU.mult))
            pending.append(lambda: epi_diff2(N, dst, e))

        def epi_diff2(N, dst, e):
            t = tmp; r = r_tmp
            sbk = ring["s"] % 4; ring["s"] += 1
            cx.op("pe", [r_sqb], [r_ps[sbk]],
                  lambda h: h.matmul(PS[sbk][:, 0:N], lhsT=ones_bf[:, :], rhs=sqb[:, 0:N], start=True, stop=True))
            cx.op("act", [r_ps[sbk]], [r[5]],
                  lambda h: h.activation(out=t[5][:, 0:N], in_=PS[sbk][:, 0:N], func=AF.Ln, bias=eps_t[:, 0:1],
                                         scale=1.0 / 128))
            cx.op("act", [r[5]], [r[5]],
                  lambda h: h.activation(out=t[5][:, 0:N], in_=t[5][:, 0:N], func=AF.Exp, scale=-0.5))
            cx.op("dve", [r[4], r[5], r_c], [s_obf[e]],
                  lambda h: h.scalar_tensor_tensor(out=obf[e][:, 0:N], in0=t[4][:, 0:N], scalar=subs[:, 0:1],
                                                   in1=t[5][:, 0:N], op0=ALU.mult, op1=ALU.mult))
            cx.dma("sp", s_obf[e], [s_obf[e]], [], dst, obf[e][:, 0:N])

        qtiles = [(t0, 512) for t0 in range(0, S, 512)] + ([] if last else [(S, CL)])

        if kind == 0:
            sc = 64 ** -0.5
            nq = 0
            def load_kv0(g):
                kb = g % 2
                cx.dma("sp", s_K[kb], [], [s_K[kb]], K[kb][:, :], kS[g * 128:(g + 1) * 128, :])
                cx.dma("sp", s_V[kb], [], [s_V[kb]], V[kb][:, :, 0:64],
                       vS[:, g * 64:(g + 1) * 64].rearrange("(j p) e -> p j e", p=128))

            def load_q0(c):
                cx.dma("sp", s_Q[c % 2], [], [s_Q[c % 2]], Q[c % 2][:, :], qS[c * 128:(c + 1) * 128, :])

            load_kv0(0)
            load_q0(0)
            for g in range(4):
                kb = g % 2
                if g + 1 < 4:
                    load_kv0(g + 1)
                for cc in range(2):
                    c = 2 * g + cc
                    qb = nq % 2; nq += 1
                    if c + 1 < 8:
                        load_q0(c + 1)
                    for hf in range(2):
                        hd = 2 * c + hf
                        psl = slice(64 * hf, 64 * hf + 64)
                        for (t0, N0) in qtiles:
                            obk = 4 + (ring["o"] % 2); ring["o"] += 1
                            rd = [s_K[kb], s_Q[qb]]
                            steps = []
                            for i in range(2):
                                steps.append(dict(k=K[kb][psl, S + 128 * i:S + 128 * (i + 1)], q=Q[qb][psl, t0:t0 + N0],
                                                  N=N0, scale=sc, rd=rd,
                                                  pv=[(obk, 0, V[kb][:, 32 + i, :], i == 0, False, [s_V[kb]])]))
                            if t0 < S:
                                for j in range(6):
                                    kp0 = t0 - 128 + 128 * j
                                    if kp0 < 0 or kp0 >= S:
                                        continue
                                    bmin, bmax = max(0, j - 2), min(3, j)
                                    N = 128 * (bmax - bmin + 1)
                                    steps.append(dict(k=K[kb][psl, kp0:kp0 + 128],
                                                      q=Q[qb][psl, t0 + 128 * bmin:t0 + 128 * bmin + N], N=N, scale=sc, rd=rd,
                                                      mask=mask[:, 128 * (bmin - j + 2):128 * (bmin - j + 2) + N],
                                                      pv=[(obk, 128 * bmin, V[kb][:, kp0 // 128, :], False, False, [s_V[kb]])]))
                            lp = steps[-1]["pv"][0]
                            steps[-1]["pv"][0] = (lp[0], lp[1], lp[2], lp[3], True, lp[5])
                            run_unit(steps, lambda obk=obk, N0=N0, hd=hd, t0=t0: epi_aug(
                                obk, N0, hd, oS[hd * 64:(hd + 1) * 64, t0:t0 + N0]))
                            tick()
        elif kind == 2:
            sc = 96 ** -0.5
            def load_head2(hd):
                b = hd % 2
                cx.dma("sp", s_K[b], [], [s_K[b]], K[b][0:64, :], kS[hd * 128:hd * 128 + 64, :])
                cx.dma("sp", s_K[b], [], [s_K[b]], K[b][64:96, :], krS[:, :])
                cx.dma("sp", s_Q[b], [], [s_Q[b]], Q[b][0:96, :], qS[hd * 128:hd * 128 + 96, :])
                cx.dma("sp", s_V[b], [], [s_V[b]], V[b][:, :, 0:64],
                       vS[:, hd * 64:(hd + 1) * 64].rearrange("(j p) e -> p j e", p=128))

            load_head2(0)
            for hd in range(16):
                b = hd % 2
                if hd + 1 < 16:
                    load_head2(hd + 1)
                for (t0, N0) in qtiles:
                    obk = 4 + (ring["o"] % 2); ring["o"] += 1
                    rd = [s_K[b], s_Q[b]]
                    js = list(range(NKT)) if t0 < S else [32, 33]
                    steps = [dict(k=K[b][0:96, 128 * j:128 * (j + 1)], q=Q[b][0:96, t0:t0 + N0], N=N0, scale=sc, rd=rd,
                                  pv=[(obk, 0, V[b][:, j, :], j == js[0], j == js[-1], [s_V[b]])]) for j in js]
                    run_unit(steps, lambda obk=obk, N0=N0, hd=hd, t0=t0: epi_aug(
                        obk, N0, None, oS[hd * 64:(hd + 1) * 64, t0:t0 + N0]))
                    tick()
        else:
            sc = 64 ** -0.5
            def load_head1(hd):
                b = hd % 2
                cx.dma("sp", s_K[b], [], [s_K[b]], K[b][:, :], kS[hd * 128:(hd + 1) * 128, :])
                cx.dma("sp", s_Q[b], [], [s_Q[b]], Q[b][:, :], qS[hd * 128:(hd + 1) * 128, :])
                cx.dma("sp", s_V[b], [], [s_V[b]], V[b][:, :, :],
                       vS[:, hd * 128:(hd + 1) * 128].rearrange("(j p) e -> p j e", p=128))

            load_head1(0)
            for hd in range(8):
                b = hd % 2
                if hd + 1 < 8:
                    load_head1(hd + 1)
                for (t0, N0) in qtiles:
                    rd = [s_K[b], s_Q[b]]
                    js = list(range(NKT)) if t0 < S else [32, 33]
                    steps = []
                    for j in js:
                        for m in range(2):
                            psl = slice(64 * m, 64 * m + 64)
                            steps.append(dict(k=K[b][psl, 128 * j:128 * (j + 1)], q=Q[b][psl, t0:t0 + N0], N=N0, scale=sc,
                                              rd=rd, pv=[(4 + m, 0, V[b][:, j, :], j == js[0], j == js[-1], [s_V[b]]),
                                                         (6 + m, 0, ones_bf[:, :], j == js[0], j == js[-1], [])]))
                    run_unit(steps, lambda N0=N0, hd=hd, t0=t0: epi_diff(N0, oS[hd * 128:(hd + 1) * 128, t0:t0 + N0]),
                             gsize=int(os.environ.get("B_GS", 2)))
        while pending:
            pending.pop(0)()
        while pending_mid:
            pending_mid.pop(0)()
        while bg:
            bg.pop(0)()
        cx.end_phase()


def _layout_inputs(inputs, b, n_layers=4):
    f = lambda a: np.ascontiguousarray(a, dtype=np.float32)
    m = {"x": f(inputs["x"][b]), "ctx": f(inputs["ctx"][b])}
    c2 = np.stack([inputs["c"][b].reshape(8, 128).T, inputs["c_ctx"].reshape(8, 128).T], axis=-1)
    m["c2"] = f(c2)
    m.update(_consts())
    for L in range(n_layers):
        p = lambda nm: inputs[f"l{L}_{nm}"]
        m[f"l{L}_ada_w"] = f(p("ada_w"))
        m[f"l{L}_ada_b"] = f(p("ada_b").reshape(48, 128).T)
        m[f"l{L}_norms"] = f(p("norms").reshape(4, 8, 128).transpose(2, 0, 1))
        k = L % 3
        if k == 0:
            m[f"l{L}_w_qkv"] = f(p("w_qkv")); m[f"l{L}_w_o"] = f(p("w_o"))
            m[f"l{L}_sink"] = f(np.broadcast_to(p("sink")[None, :], (128, 16)))
        elif k == 1:
            m[f"l{L}_w_qkv"] = f(p("w_qkv")); m[f"l{L}_w_o"] = f(p("w_o"))
            m[f"l{L}_lam"] = f(np.broadcast_to(p("lambda").reshape(1, 256), (128, 256)))
            m[f"l{L}_subln"] = f(p("subln").reshape(128, 1))
        else:
            m[f"l{L}_w_in"] = f(p("w_in")); m[f"l{L}_w_uq"] = f(p("w_uq")); m[f"l{L}_w_ukv"] = f(p("w_ukv"))
            m[f"l{L}_w_o"] = f(p("w_o"))
            m[f"l{L}_qn"] = f(p("q_norm").reshape(3, 128).T)
            m[f"l{L}_kvn"] = f(p("kv_norm").reshape(2, 128).T)
        m[f"l{L}_mlp_w1"] = f(p("mlp_w1").reshape(8, 128, 8, 512).transpose(2, 1, 0, 3).reshape(8, 128, 8 * 512))
        m[f"l{L}_mlp_w2"] = f(p("mlp_w2").reshape(32, 128, 8, 128).transpose(2, 1, 0, 3).reshape(8, 128, 32 * 128))
    return m


def kernel(**inputs):
    inputs = {k: np.asarray(v) for k, v in inputs.items()}
    nc, _ = build(4)
    in_maps = [_layout_inputs(inputs, b) for b in range(NCORES)]
    res = run_bass_kernel_spmd(nc, in_maps, core_ids=list(range(NCORES)))
    return np.stack([np.asarray(r["out"]) for r in res.results], axis=0).astype(np.float32)
```

```python
import math
import os
from contextlib import ExitStack

import numpy as np
import concourse.bass as bass
import concourse.mybir as mybir
from concourse.bass_utils import run_bass_kernel_spmd

F32 = mybir.dt.float32
BF16 = mybir.dt.bfloat16
AF = mybir.ActivationFunctionType
ALU = mybir.AluOpType

D = 1024
S = 4096
CL = 256
ST = S + CL
DFF = 4096
EPS = 1e-6
NCORES = 8


class Res:
    __slots__ = ("name", "w", "r", "sem", "cnt", "excl")

    def __init__(self, name, sem=None):
        self.name = name
        self.excl = False
        self.w = {}
        self.r = {}
        self.sem = sem
        self.cnt = 0


class Eng:
    def __init__(self, name, kind, sem, self_sync):
        self.name = name
        self.kind = kind
        self.sem = sem
        self.self_sync = self_sync
        self.base = 0
        self.nops = 0
        self.q = []
        self.confirmed = {}
        self.waited = set()
        self.rank = {}
        self.tot_ins = 0


class DSem:
    __slots__ = ("sem", "cnt")

    def __init__(self, sem):
        self.sem = sem
        self.cnt = 0


class _Rec:
    def __getattr__(self, name):
        return lambda *a, **k: (name, a, k)


_REC = _Rec()


class Ctx:
    def __init__(self, nc, stack, n_dma_sems=40):
        self.nc = nc
        self.stack = stack
        self.engs = {}
        self.all_res = []
        self.dma_pool = []
        for i in range(n_dma_sems):
            r = Res("dma%d" % i)
            r.sem = {q: DSem(stack.enter_context(nc.semaphore("dma%d%s" % (i, q)))) for q in ("sp", "pool")}
            self.dma_pool.append(r)
        self.dma_used = 0

    def add_engine(self, name, kind, self_sync=True):
        e = Eng(name, kind, self.stack.enter_context(self.nc.semaphore("e_" + name)), self_sync)
        self.engs[name] = e
        return e

    def res(self, name):
        r = Res(name)
        self.all_res.append(r)
        return r

    def slot(self, name):
        r = self.dma_pool[self.dma_used]
        self.dma_used += 1
        r.name = name
        return r

    def _deps(self, e, reads, writes):
        need = {}
        for r in reads:
            for k, ev in r.w.items():
                if k not in need or need[k][2] < ev[2]:
                    need[k] = ev
            if r.excl:
                for k, ev in r.r.items():
                    if k != e.name and (k not in need or need[k][2] < ev[2]):
                        need[k] = ev
        for w in writes:
            for d in (w.w, w.r):
                for k, ev in d.items():
                    if k not in need or need[k][2] < ev[2]:
                        need[k] = ev
        for k, ev in need.items():
            typ, obj, val = ev
            if typ == "e" and obj is e and not e.self_sync:
                continue
            if e.confirmed.get(k, 0) < val:
                e.q.append(("wait", typ, obj, val))
                e.confirmed[k] = val
                if typ == "e":
                    obj.waited.add(val)

    @staticmethod
    def _post(key, ev, reads, writes):
        for r in reads:
            r.r[key] = ev
        for w in writes:
            w.w = {key: ev}
            w.r = {}

    def op(self, eng, reads, writes, fn):
        e = self.engs[eng]
        self._deps(e, reads, writes)
        e.nops += 1
        fn = fn(_REC)
        e.q.append(("op", fn, e.nops))
        self._post(e.name, ("e", e, e.nops), reads, writes)

    def dma(self, eng, slot, reads, writes, out, in_):
        e = self.engs[eng]
        self._deps(e, reads, writes)
        ds = slot.sem["pool" if eng == "pool" else "sp"]
        ds.cnt += 16
        e.q.append(("dma", out, in_, ds))
        self._post(id(ds), ("d", ds, ds.cnt), reads, writes)

    def end_phase(self):
        for e in self.engs.values():
            for o in self.engs.values():
                if o is e or o.nops == 0:
                    continue
                if e.confirmed.get(o.name, 0) < o.nops:
                    e.q.append(("wait", "e", o, o.nops))
                    e.confirmed[o.name] = o.nops
                    o.waited.add(o.nops)
            for sl in self.dma_pool[:self.dma_used]:
                for s in sl.sem.values():
                    if s.cnt and e.confirmed.get(id(s), 0) < s.cnt:
                        e.q.append(("wait", "d", s, s.cnt))
                        e.confirmed[id(s)] = s.cnt
        for e in self.engs.values():
            srt = sorted(e.waited)
            e.rank = {idx: e.base + i + 1 for i, idx in enumerate(srt)}
        with self.nc.Block() as block:
            for e in self.engs.values():
                def body(h, e=e):
                    for it in e.q:
                        if it[0] == "wait":
                            _, typ, obj, val = it
                            if typ == "e":
                                h.wait_ge(obj.sem, obj.rank[val])
                            else:
                                h.wait_ge(obj.sem, val)
                        elif it[0] == "op":
                            name, a, k = it[1]
                            ins = getattr(h, name)(*a, **k)
                            if it[2] in e.waited:
                                ins.then_inc(e.sem, 1)
                        else:
                            h.dma_start(out=it[1], in_=it[2]).then_inc(it[3].sem, 16)
                getattr(block, e.kind)(body)
        for e in self.engs.values():
            e.tot_ins += len(e.q)
            e.base += len(e.waited)
            e.nops = 0
            e.q = []
            e.waited = set()
            e.rank = {}
            e.confirmed = {k: v for k, v in e.confirmed.items() if not isinstance(k, str)}
        for r in self.all_res:
            r.w = {}
            r.r = {}
        self.all_res = []
        for s in self.dma_pool:
            s.w = {}
            s.r = {}
        self.dma_used = 0


def _rope_tables(d, nrep, pad_lo):
    da = d // 2
    inv = 10000.0 ** (-np.arange(0, da, 2, dtype=np.float32) / da)
    t = np.arange(S)
    rows = (t // 64).astype(np.float32)
    cols = (t % 64).astype(np.float32)
    q = da // 2
    C = np.zeros((d, S), np.float32)
    Sg = np.zeros((d, S), np.float32)
    partner = np.zeros(d, np.int64)
    for half, pos in enumerate((rows, cols)):
        ang = (pos[None, :].astype(np.float32) * inv[:, None].astype(np.float32)).astype(np.float32)
        c = np.cos(ang).astype(np.float32)
        s = np.sin(ang).astype(np.float32)
        b = half * da
        C[b:b + q] = c
        C[b + q:b + da] = c
        Sg[b:b + q] = -s
        Sg[b + q:b + da] = s
        partner[b:b + q] = np.arange(b + q, b + da)
        partner[b + q:b + da] = np.arange(b, b + q)
    n = pad_lo + nrep * d
    Cf = np.ones((n, S), np.float32)
    Sf = np.zeros((n, S), np.float32)
    P = np.zeros((n, n), np.float32)
    for r in range(nrep):
        o = pad_lo + r * d
        Cf[o:o + d] = C
        Sf[o:o + d] = Sg
        for m in range(d):
            P[o + partner[m], o + m] = 1.0
    return Cf, Sf, P


def _consts():
    c64, s64, p64 = _rope_tables(64, 2, 0)
    c96, s96, p96 = _rope_tables(32, 1, 64)
    kk = np.arange(128)[:, None]
    qq = np.arange(128)[None, :]
    mask = np.concatenate([(qq >= kk), np.ones((128, 128), bool), (qq <= kk)], axis=1).astype(np.float32)
    return {
        "k_ident": np.eye(128, dtype=np.float32),
        "k_c64": c64, "k_s64": s64, "k_p64": p64,
        "k_c96": c96, "k_s96": s96, "k_p96": p96,
        "k_mask": mask,
    }


LAYER_W = {
    0: [("w_qkv", [D, 1536]), ("sink", [128, 16]), ("w_o", [D, D])],
    1: [("w_qkv", [D, 3072]), ("lam", [128, 256]), ("subln", [128, 1]), ("w_o", [D, D])],
    2: [("w_in", [D, 672]), ("qn", [128, 3]), ("kvn", [128, 2]), ("w_uq", [384, 1536]), ("w_ukv", [256, 2048]),
        ("w_o", [D, D])],
}


def build(n_layers=4, dbg=False, stop_after=None):
    nc = bass.Bass("TRN2", target_bir_lowering=False)
    I = {}

    def din(name, shape, dt=F32):
        I[name] = nc.dram_tensor(name, list(shape), dt, kind="ExternalInput").ap()
        return I[name]

    din("x", [S, D]); din("ctx", [CL, D]); din("c2", [128, 8, 2])
    for k, v in _consts().items():
        din(k, v.shape)
    for L in range(n_layers):
        din(f"l{L}_ada_w", [D, 6 * D]); din(f"l{L}_ada_b", [128, 48]); din(f"l{L}_norms", [128, 4, 8])
        for nm, shp in LAYER_W[L % 3]:
            din(f"l{L}_{nm}", shp)
        din(f"l{L}_mlp_w1", [8, 128, 8 * 512]); din(f"l{L}_mlp_w2", [8, 128, 32 * 128])
    out = nc.dram_tensor("out", [S, D], F32, kind="ExternalOutput").ap()
    hT = nc.dram_tensor("hT", [D, ST], F32).ap()
    qS = nc.dram_tensor("qS", [16 * 128, ST], BF16).ap()
    kS = nc.dram_tensor("kS", [16 * 128, ST], BF16).ap()
    krS = nc.dram_tensor("krS", [32, ST], BF16).ap()
    vS = nc.dram_tensor("vS", [ST, 1024], BF16).ap()
    oS = nc.dram_tensor("oS", [D, ST], BF16).ap()
    dbg_out = {}
    if dbg:
        for nm, src in (("d_hT", hT), ("d_oS", oS), ("d_qS", qS), ("d_kS", kS), ("d_vS", vS), ("d_krS", krS)):
            dbg_out[nm] = (nc.dram_tensor(nm, list(src.shape), src.dtype, kind="ExternalOutput").ap(), src)
        dbg_mv = nc.dram_tensor("d_mv", [128, 96], F32, kind="ExternalOutput").ap()

    TILES = [(t0, 512, 0) for t0 in range(0, S, 512)] + [(S, CL, 1)]

    with ExitStack() as gs:
        cx = Ctx(nc, gs)
        for n, k in (("sp", "sync"), ("pe", "tensor"), ("act", "scalar"), ("dve", "vector"), ("pool", "gpsimd")):
            cx.add_engine(n, k, self_sync=(n != "pe"))

        uid = [0]

        def sbuf(st, name, shape, dt):
            uid[0] += 1
            return st.enter_context(nc.sbuf_tensor("%s_%d" % (name, uid[0]), list(shape), dt))

        ones_bf = sbuf(gs, "ones_bf", [128, 128], BF16)
        ident = sbuf(gs, "ident", [128, 128], F32)
        eps_t = sbuf(gs, "eps_t", [128, 1], F32)
        sc2 = sbuf(gs, "sc2", [128, 8, 2], BF16)
        MVall = sbuf(gs, "MV", [128, 4, 6, 2, 8], F32)
        PS = [gs.enter_context(nc.psum_tensor("ps%d" % i, [128, 512], F32)) for i in range(8)]

        def psres():
            rr = [cx.res("ps%d" % i) for i in range(8)]
            for r_ in rr:
                r_.excl = True
            return rr

        def load_w(slot, dst, src, nk, eng="pool"):
            for k in range(nk):
                cx.dma(eng, slot, [], [slot], dst[:, k, :], src[k * 128:(k + 1) * 128, :])

        def m_chunks(Ls, st, bank, r_ps):
            if not Ls:
                return []
            wa = [sbuf(st, "wa%d" % i, [128, 8, 1024], BF16) for i in range(2)]
            s_wa = [cx.slot("wa%d" % i) for i in range(2)]
            adab = sbuf(st, "adab", [128, 48], F32); s_adab = cx.slot("adab")
            gn = sbuf(st, "gn", [128, 4, 8], F32); s_gn = cx.slot("gn")
            mod = sbuf(st, "mod", [128, 48, 2], F32)
            out = []
            for L in Ls:
                out += m_chunks1(L, bank, r_ps, wa, s_wa, adab, s_adab, gn, s_gn, mod)
            return out

        def m_chunks1(L, bank, r_ps, wa, s_wa, adab, s_adab, gn, s_gn, mod):
            W = lambda nm: I[f"l{L}_{nm}"]
            MVl = MVall[:, L]
            r_mv = cx.res("mv%d" % L); r_t = cx.res("mt%d" % L)
            chunks = []

            def ld(m):
                load_w(s_wa[m % 2], wa[m % 2], W("ada_w")[:, m * 1024:(m + 1) * 1024], 8)

            def first():
                cx.dma("sp", s_adab, [], [s_adab], adab[:], W("ada_b")[:, :])
                cx.dma("sp", s_gn, [], [s_gn], gn[:], W("norms")[:, :, :])
                ld(0)
                ld(1)
            chunks.append(first)

            def blk(m):
                b = m % 2
                for jb in range(8):
                    j = m * 8 + jb
                    for dc in range(8):
                        cx.op("pe", [s_wa[b]], [r_ps[bank]],
                              lambda h, b=b, jb=jb, j=j, dc=dc: h.matmul(
                                  PS[bank][:, 2 * j:2 * j + 2], lhsT=wa[b][:, dc, jb * 128:(jb + 1) * 128],
                                  rhs=sc2[:, dc, 0:2], start=(dc == 0), stop=(dc == 7), skip_group_check=True))
                if m + 2 < 6:
                    ld(m + 2)
            for m in range(6):
                chunks.append(lambda m=m: blk(m))

            def fin():
                cx.op("dve", [r_ps[bank], s_adab], [r_t],
                      lambda h: h.tensor_tensor(out=mod[:, :, :], in0=PS[bank][:, 0:96].rearrange("p (j s) -> p j s", s=2),
                                                in1=adab[:, :].unsqueeze(2).to_broadcast([128, 48, 2]), op=ALU.add))
                for s_ in range(2):
                    for (dst, scale_j, g_i) in ((0, 8, 0), (3, 32, 2)):
                        cx.op("dve", [r_t, s_gn], [r_mv],
                              lambda h, s_=s_, dst=dst, scale_j=scale_j, g_i=g_i: h.scalar_tensor_tensor(
                                  out=MVl[:, dst, s_, :], in0=mod[:, scale_j:scale_j + 8, s_], scalar=1.0,
                                  in1=gn[:, g_i, :], op0=ALU.add, op1=ALU.mult))
                    for (dst, shift_j) in ((1, 0), (4, 24)):
                        cx.op("dve", [r_t], [r_mv],
                              lambda h, s_=s_, dst=dst, shift_j=shift_j: h.tensor_copy(
                                  out=MVl[:, dst, s_, :], in_=mod[:, shift_j:shift_j + 8, s_]))
                    for (dst, gate_j, g_i) in ((2, 16, 1), (5, 40, 3)):
                        cx.op("dve", [r_t, s_gn], [r_mv],
                              lambda h, s_=s_, dst=dst, gate_j=gate_j, g_i=g_i: h.tensor_tensor(
                                  out=MVl[:, dst, s_, :], in0=mod[:, gate_j:gate_j + 8, s_], in1=gn[:, g_i, :],
                                  op=ALU.mult))
            chunks.append(fin)
            return chunks


        with ExitStack() as st:
            r_c = cx.res("c")
            r_ps = psres()
            c2f = sbuf(st, "c2f", [128, 8, 2], F32)
            s_c2 = cx.slot("c2f"); s_id = cx.slot("ident")
            cx.dma("sp", s_c2, [], [s_c2], c2f[:], I["c2"][:, :, :])
            cx.dma("sp", s_id, [], [s_id], ident[:], I["k_ident"][:, :])
            cx.op("dve", [], [r_c], lambda h: h.memset(ones_bf[:], 1.0))
            cx.op("dve", [], [r_c], lambda h: h.memset(eps_t[:], EPS))
            cx.op("act", [s_c2], [r_c], lambda h: h.activation(out=sc2[:], in_=c2f[:], func=AF.Silu))
            xs = [sbuf(st, "xs%d" % i, [128, 4, D], F32) for i in range(2)]
            s_xs = [cx.slot("xs%d" % i) for i in range(2)]
            hts = [sbuf(st, "xh%d" % i, [128, 8, 512], F32) for i in range(2)]
            s_hts = [cx.slot("xh%d" % i) for i in range(2)]
            bg0 = None
            for ti, (t0, T, strm) in enumerate(TILES):
                if bg0 is None:
                    bg0 = m_chunks([0], st, 7, r_ps)
                if bg0:
                    bg0.pop(0)()
                b = ti % 2
                ns = T // 128
                src = I["x"][t0:t0 + T, :] if strm == 0 else I["ctx"][:, :]
                cx.dma("sp", s_xs[b], [], [s_xs[b]], xs[b][:, 0:ns, :], src.rearrange("(s p) d -> p s d", p=128))
                for c in range(8):
                    pb = c % 4
                    for s_ in range(ns):
                        cx.op("pe", [s_xs[b], s_id], [r_ps[pb]],
                              lambda h, pb=pb, s_=s_, c=c, b=b: h.transpose(
                                  out=PS[pb][:, s_ * 128:(s_ + 1) * 128], in_=xs[b][:, s_, c * 128:(c + 1) * 128],
                                  identity=ident[:]))
                    if c % 2 == 0:
                        cx.op("act", [r_ps[pb]], [s_hts[b]],
                              lambda h, pb=pb, c=c, b=b, T=T: h.copy(out=hts[b][:, c, 0:T], in_=PS[pb][:, 0:T]))
                    else:
                        cx.op("dve", [r_ps[pb]], [s_hts[b]],
                              lambda h, pb=pb, c=c, b=b, T=T: h.tensor_copy(out=hts[b][:, c, 0:T], in_=PS[pb][:, 0:T]))
                cx.dma("pool", s_hts[b], [s_hts[b]], [],
                       hT[:, t0:t0 + T].rearrange("(c p) t -> p c t", p=128), hts[b][:, :, 0:T])
            while bg0:
                bg0.pop(0)()
            cx.end_phase()

        def rstd_from_psum(ps_i, r_ps, rstd, r_rstd, T, nfeat):
            cx.op("act", [r_ps[ps_i]], [r_rstd],
                  lambda h: h.activation(out=rstd[:, 0:T], in_=PS[ps_i][:, 0:T], func=AF.Ln, bias=eps_t[:, 0:1],
                                         scale=1.0 / nfeat))
            cx.op("act", [r_rstd], [r_rstd],
                  lambda h: h.activation(out=rstd[:, 0:T], in_=rstd[:, 0:T], func=AF.Exp, scale=-0.5))

        def norm_front(st_res, ht, T, A, Bv, u, r_u, sq, r_sq, un, r_un, rstd, r_rstd, ps_i, r_ps):
            cx.op("act", [st_res], [r_sq], lambda h: h.activation(out=sq[:, :, 0:T], in_=ht[:, :, 0:T], func=AF.Square))
            for c in range(8):
                cx.op("pe", [r_sq], [r_ps[ps_i]],
                      lambda h, c=c: h.matmul(PS[ps_i][:, 0:T], lhsT=ones_bf[:, :], rhs=sq[:, c, 0:T],
                                              start=(c == 0), stop=(c == 7)))
            rstd_from_psum(ps_i, r_ps, rstd, r_rstd, T, D)
            cx.op("dve", [st_res, r_rstd], [r_un],
                  lambda h: h.tensor_tensor(out=un[:, :, 0:T], in0=ht[:, :, 0:T],
                                            in1=rstd[:, 0:T].unsqueeze(1).to_broadcast([128, 8, T]), op=ALU.mult))
            for c in range(8):
                if c % 2 == 0:
                    cx.op("act", [r_un], [r_u],
                          lambda h, c=c: h.activation(out=u[:, c, 0:T], in_=un[:, c, 0:T], func=AF.Identity,
                                                      bias=Bv[:, c:c + 1], scale=A[:, c:c + 1]))
                else:
                    cx.op("dve", [r_un], [r_u],
                          lambda h, c=c: h.tensor_scalar(out=u[:, c, 0:T], in0=un[:, c, 0:T], scalar1=A[:, c:c + 1],
                                                         scalar2=Bv[:, c:c + 1], op0=ALU.mult, op1=ALU.add))

        def resid_back(y, r_y, ht, r_ht, T, G, sq, r_sq, rstd, r_rstd, ps_i, r_ps):
            cx.op("act", [r_y], [r_sq], lambda h: h.activation(out=sq[:, :, 0:T], in_=y[:, :, 0:T], func=AF.Square))
            for c in range(8):
                cx.op("pe", [r_sq], [r_ps[ps_i]],
                      lambda h, c=c: h.matmul(PS[ps_i][:, 0:T], lhsT=ones_bf[:, :], rhs=sq[:, c, 0:T],
                                              start=(c == 0), stop=(c == 7)))
            rstd_from_psum(ps_i, r_ps, rstd, r_rstd, T, D)
            for c in range(8):
                cx.op("dve", [r_y, r_rstd], [r_y],
                      lambda h, c=c: h.tensor_tensor(out=y[:, c, 0:T], in0=y[:, c, 0:T], in1=rstd[:, 0:T], op=ALU.mult))
                cx.op("dve", [r_y, r_ht], [r_ht],
                      lambda h, c=c: h.scalar_tensor_tensor(out=ht[:, c, 0:T], in0=y[:, c, 0:T], scalar=G[:, c:c + 1],
                                                            in1=ht[:, c, 0:T], op0=ALU.mult, op1=ALU.add))

        for L in range(n_layers):
            kind = L % 3
            last = (L == n_layers - 1)
            W = lambda nm: I[f"l{L}_{nm}"]
            MV = MVall[:, L]
            phase_A(nc, cx, I, L, kind, last, TILES, sbuf, psres, PS, MV, ones_bf, eps_t, norm_front, load_w,
                    hT, qS, kS, krS, vS)
            if stop_after == ("A", L):
                break
            bgL = [x for x in {0: [1, 2], 2: [3]}.get(L, []) if x < n_layers]
            phase_B(nc, cx, I, L, kind, last, sbuf, psres, PS, ones_bf, eps_t, qS, kS, krS, vS, oS,
                    bg_make=(lambda st_, r_ps_, bgL=bgL: m_chunks(bgL, st_, 6, r_ps_)), bg_every=(9 if kind == 0 else 4))
            if stop_after == ("B", L):
                break
            with ExitStack() as st:
                r_ps = psres()
                wo = sbuf(st, "wo", [128, 8, D], BF16); s_wo = cx.slot("wo")
                load_w(s_wo, wo, W("w_o"), 8)
                ot = [sbuf(st, "ot%d" % i, [128, 8, 512], BF16) for i in range(3)]
                s_ot = [cx.slot("ot%d" % i) for i in range(3)]
                ht = [sbuf(st, "cht%d" % i, [128, 8, 512], F32) for i in range(3)]
                s_ht = [cx.slot("cht%d" % i) for i in range(3)]
                y = [sbuf(st, "cy%d" % i, [128, 8, 512], F32) for i in range(2)]; r_y = [cx.res("y%d" % i) for i in range(2)]
                sq = [sbuf(st, "csq%d" % i, [128, 8, 512], BF16) for i in range(2)]; r_sq = [cx.res("sq%d" % i) for i in range(2)]
                rstd = [sbuf(st, "crstd%d" % i, [128, 512], F32) for i in range(2)]; r_rstd = [cx.res("rstd%d" % i) for i in range(2)]
                tiles = TILES[:-1] if last else TILES
                cpend = []

                def c_tail(ti):
                    t0, T, strm = tiles[ti]
                    b = ti % 2
                    b3 = ti % 3
                    G = MV[:, 2, strm, :]
                    for c in range(8):
                        cx.op("pe", [r_sq[b]], [r_ps[0]],
                              lambda h, c=c: h.matmul(PS[0][:, 0:T], lhsT=ones_bf[:, :], rhs=sq[b][:, c, 0:T],
                                                      start=(c == 0), stop=(c == 7)))
                    rstd_from_psum(0, r_ps, rstd[b], r_rstd[b], T, D)
                    for c in range(8):
                        cx.op("dve", [r_y[b], r_rstd[b]], [r_y[b]],
                              lambda h, c=c: h.tensor_tensor(out=y[b][:, c, 0:T], in0=y[b][:, c, 0:T], in1=rstd[b][:, 0:T],
                                                             op=ALU.mult))
                        cx.op("dve", [r_y[b], s_ht[b3]], [s_ht[b3]],
                              lambda h, c=c: h.scalar_tensor_tensor(out=ht[b3][:, c, 0:T], in0=y[b][:, c, 0:T],
                                                                    scalar=G[:, c:c + 1], in1=ht[b3][:, c, 0:T],
                                                                    op0=ALU.mult, op1=ALU.add))
                    cx.dma("pool", s_ht[b3], [s_ht[b3]], [],
                           hT[:, t0:t0 + T].rearrange("(c p) t -> p c t", p=128), ht[b3][:, :, 0:T])

                for ti, (t0, T, strm) in enumerate(tiles):
                    b = ti % 2
                    b3 = ti % 3
                    cx.dma("sp", s_ot[b3], [], [s_ot[b3]], ot[b3][:, :, 0:T],
                           oS[:, t0:t0 + T].rearrange("(c p) t -> p c t", p=128))
                    cx.dma("sp", s_ht[b3], [], [s_ht[b3]], ht[b3][:, :, 0:T],
                           hT[:, t0:t0 + T].rearrange("(c p) t -> p c t", p=128))
                    for c in range(8):
                        pb = 1 + (c % 4)
                        for k in range(8):
                            cx.op("pe", [s_wo, s_ot[b3]], [r_ps[pb]],
                                  lambda h, pb=pb, k=k, c=c, b3=b3, T=T: h.matmul(
                                      PS[pb][:, 0:T], lhsT=wo[:, k, c * 128:(c + 1) * 128], rhs=ot[b3][:, k, 0:T],
                                      start=(k == 0), stop=(k == 7)))
                        cx.op("act", [r_ps[pb]], [r_y[b]],
                              lambda h, pb=pb, c=c, T=T, b=b: h.copy(out=y[b][:, c, 0:T], in_=PS[pb][:, 0:T]))
                    cx.op("act", [r_y[b]], [r_sq[b]],
                          lambda h, b=b, T=T: h.activation(out=sq[b][:, :, 0:T], in_=y[b][:, :, 0:T], func=AF.Square))
                    while cpend:
                        cpend.pop(0)()
                    cpend.append(lambda ti=ti: c_tail(ti))
                while cpend:
                    cpend.pop(0)()
                cx.end_phase()
            if stop_after == ("C", L):
                break

            with ExitStack() as st:
                r_ps = psres()
                TD = 1024
                dtiles = [(t0, TD, 0) for t0 in range(0, S, TD)] + ([] if last else [(S, CL, 1)])
                ht = sbuf(st, "dht", [128, 8, TD], F32); s_hth = [cx.slot("dht0"), cx.slot("dht1")]
                u = sbuf(st, "du", [128, 8, TD], BF16); r_uh = [cx.res("u0"), cx.res("u1")]
                sq = sbuf(st, "dsq", [128, 8, 512], BF16); r_sq = cx.res("sq")
                y = sbuf(st, "dy", [128, 8, TD], F32); r_y = [cx.res("y0"), cx.res("y1")]
                rstd = sbuf(st, "drstd", [128, 512], F32); r_rstd = cx.res("rstd")
                hid = sbuf(st, "hid", [128, 32, TD], BF16); r_hid = [cx.res("hid%d" % j) for j in range(32)]
                rlt = sbuf(st, "rl", [128, 1024], F32)
                rl = [rlt[:, 0:512], rlt[:, 512:1024]]; r_rl = [cx.res("rl%d" % i) for i in range(2)]
                fuse_out = last and n_layers == 4 and (not dbg) and stop_after is None
                s_ost = cx.slot("ost")
                ntr = 0
                w1 = [sbuf(st, "w1_%d" % i, [128, 8, 512], BF16) for i in range(3)]
                s_w1 = [cx.slot("w1_%d" % i) for i in range(3)]
                w2 = [sbuf(st, "w2_%d" % i, [128, 32, 128], BF16) for i in range(2)]
                s_w2 = [cx.slot("w2_%d" % i) for i in range(2)]
                nw1 = 0
                nw2 = 0
                nev = 0
                def halves_of(T):
                    return [(o, min(512, T - o)) for o in range(0, T, 512)]

                def load_half(ti, hi):
                    t0_, T_, _ = dtiles[ti]
                    o_, Th_ = halves_of(T_)[hi]
                    cx.dma("sp", s_hth[hi], [], [s_hth[hi]], ht[:, :, o_:o_ + Th_],
                           hT[:, t0_ + o_:t0_ + o_ + Th_].rearrange("(c p) t -> p c t", p=128))

                for hi in range(len(halves_of(dtiles[0][1]))):
                    load_half(0, hi)
                for ti, (t0, T, strm) in enumerate(dtiles):
                    halves = halves_of(T)
                    for hi, (o, Th) in enumerate(halves):
                        norm_front(s_hth[hi], ht[:, :, o:o + Th], Th, MV[:, 3, strm, :], MV[:, 4, strm, :],
                                   u[:, :, o:o + Th], r_uh[hi], sq, r_sq, y[:, :, o:o + Th], r_y[hi], rstd, r_rstd, 0, r_ps)
                    for jp in range(4):
                        bs = []
                        for jg in (2 * jp, 2 * jp + 1):
                            b = nw1 % 3
                            nw1 += 1
                            bs.append((jg, b))
                            cx.dma("pool", s_w1[b], [], [s_w1[b]], w1[b][:, :, :].rearrange("p k f -> p (k f)"),
                                   W("mlp_w1")[jg, :, :])
                        for hi, (o, Th) in enumerate(halves):
                            for (jg, b) in bs:
                                for jj in range(4):
                                    j = jg * 4 + jj
                                    pb = 1 + (nev % 4)
                                    for k in range(8):
                                        cx.op("pe", [s_w1[b], r_uh[hi]], [r_ps[pb]],
                                              lambda h, pb=pb, b=b, k=k, jj=jj, o=o, Th=Th: h.matmul(
                                                  PS[pb][:, 0:Th], lhsT=w1[b][:, k, jj * 128:(jj + 1) * 128],
                                                  rhs=u[:, k, o:o + Th], start=(k == 0), stop=(k == 7)))
                                    rb = nev % 2
                                    nev += 1
                                    cx.op("act", [r_ps[pb]], [r_rl[rb]],
                                          lambda h, pb=pb, rb=rb, Th=Th: h.activation(out=rl[rb][:, 0:Th],
                                                                                      in_=PS[pb][:, 0:Th], func=AF.Relu))
                                    cx.op("dve", [r_rl[rb]], [r_hid[j]],
                                          lambda h, rb=rb, j=j, o=o, Th=Th: h.tensor_tensor(
                                              out=hid[:, j, o:o + Th], in0=rl[rb][:, 0:Th], in1=rl[rb][:, 0:Th],
                                              op=ALU.mult))
                    for c in range(8):
                        b = nw2 % 2
                        nw2 += 1
                        cx.dma("pool", s_w2[b], [], [s_w2[b]], w2[b][:, :, :].rearrange("p j f -> p (j f)"),
                               W("mlp_w2")[c, :, :])
                        for hi, (o, Th) in enumerate(halves):
                            pb = 5 + ((2 * c + hi) % 3)
                            for j in range(32):
                                cx.op("pe", [s_w2[b], r_hid[j]], [r_ps[pb]],
                                      lambda h, pb=pb, b=b, j=j, o=o, Th=Th: h.matmul(
                                          PS[pb][:, 0:Th], lhsT=w2[b][:, j, :], rhs=hid[:, j, o:o + Th],
                                          start=(j == 0), stop=(j == 31)))
                            if (c + hi) % 2 == 0:
                                cx.op("act", [r_ps[pb]], [r_y[hi]],
                                      lambda h, pb=pb, c=c, o=o, Th=Th: h.copy(out=y[:, c, o:o + Th], in_=PS[pb][:, 0:Th]))
                            else:
                                cx.op("dve", [r_ps[pb]], [r_y[hi]],
                                      lambda h, pb=pb, c=c, o=o, Th=Th: h.tensor_copy(out=y[:, c, o:o + Th], in_=PS[pb][:, 0:Th]))
                    for hi, (o, Th) in enumerate(halves):
                        resid_back(y[:, :, o:o + Th], r_y[hi], ht[:, :, o:o + Th], s_hth[hi], Th, MV[:, 5, strm, :], sq, r_sq,
                                   rstd, r_rstd, 0, r_ps)
                        if fuse_out:
                            for s_ in range(Th // 128):
                                for hb in range(2):
                                    pb = (5, 6, 7)[ntr % 3]
                                    ntr += 1
                                    for cc in range(4):
                                        c = hb * 4 + cc
                                        cx.op("pe", [s_hth[hi]], [r_ps[pb]],
                                              lambda h, pb=pb, cc=cc, c=c, s_=s_, o=o: h.transpose(
                                                  out=PS[pb][:, cc * 128:(cc + 1) * 128],
                                                  in_=ht[:, c, o + s_ * 128:o + (s_ + 1) * 128], identity=ident[:]))
                                    if hb == 0:
                                        cx.op("act", [r_ps[pb]], [r_rl[0], s_ost],
                                              lambda h, pb=pb: h.copy(out=rlt[:, 0:512], in_=PS[pb][:, :]))
                                    else:
                                        cx.op("dve", [r_ps[pb]], [r_rl[1], s_ost],
                                              lambda h, pb=pb: h.tensor_copy(out=rlt[:, 512:1024], in_=PS[pb][:, :]))
                                cx.dma("sp", s_ost, [s_ost, r_rl[0], r_rl[1]], [],
                                       out[t0 + o + s_ * 128:t0 + o + (s_ + 1) * 128, :], rlt[:, :])
                        else:
                            cx.dma("sp", s_hth[hi], [s_hth[hi]], [],
                                   hT[:, t0 + o:t0 + o + Th].rearrange("(c p) t -> p c t", p=128), ht[:, :, o:o + Th])
                        if ti + 1 < len(dtiles) and hi < len(halves_of(dtiles[ti + 1][1])):
                            load_half(ti + 1, hi)
                cx.end_phase()
            if stop_after == ("D", L):
                break

        with ExitStack() as st:
            r_ps = psres()
            hts = [sbuf(st, "yh%d" % i, [128, 8, 512], F32) for i in range(2)]
            s_hts = [cx.slot("yh%d" % i) for i in range(2)]
            os_ = [sbuf(st, "yo%d" % i, [128, 4, D], F32) for i in range(2)]
            s_os = [cx.slot("yo%d" % i) for i in range(2)]
            s_id = cx.slot("ident2")
            fused_out = (n_layers == 4) and (not dbg) and stop_after is None
            for ti, (t0, T, strm) in enumerate([] if fused_out else TILES[:-1]):
                b = ti % 2
                cx.dma("sp", s_hts[b], [], [s_hts[b]], hts[b][:, :, :], hT[:, t0:t0 + T].rearrange("(c p) t -> p c t", p=128))
                for s_ in range(4):
                    for half in range(2):
                        pb = (s_ * 2 + half) % 4
                        for cc in range(4):
                            c = half * 4 + cc
                            cx.op("pe", [s_hts[b]], [r_ps[pb]],
                                  lambda h, pb=pb, cc=cc, c=c, s_=s_, b=b: h.transpose(
                                      out=PS[pb][:, cc * 128:(cc + 1) * 128], in_=hts[b][:, c, s_ * 128:(s_ + 1) * 128],
                                      identity=ident[:]))
                        if half == 0:
                            cx.op("act", [r_ps[pb]], [s_os[b]],
                                  lambda h, pb=pb, s_=s_, b=b: h.copy(out=os_[b][:, s_, 0:512], in_=PS[pb][:, :]))
                        else:
                            cx.op("dve", [r_ps[pb]], [s_os[b]],
                                  lambda h, pb=pb, s_=s_, b=b: h.tensor_copy(out=os_[b][:, s_, 512:1024], in_=PS[pb][:, :]))
                cx.dma("pool", s_os[b], [s_os[b]], [], out[t0:t0 + T, :].rearrange("(s p) d -> p s d", p=128), os_[b][:, :, :])
            if dbg:
                s_dd = cx.slot("dbgcopy")
                for nm, (dst, src) in dbg_out.items():
                    nr = src.shape[0]
                    for i in range(0, nr, 256):
                        cx.dma("sp", s_dd, [], [], dst[i:min(nr, i + 256)], src[i:min(nr, i + 256)])
            cx.end_phase()
        stats = {n: e.tot_ins for n, e in cx.engs.items()}
        stats["sem_base"] = {n: e.base for n, e in cx.engs.items()}
        stats["dma_max"] = max(d_.cnt for s in cx.dma_pool for d_ in s.sem.values())
    return nc, stats


def phase_A(nc, cx, I, L, kind, last, TILES, sbuf, psres, PS, MV, ones_bf, eps_t, norm_front, load_w,
            hT, qS, kS, krS, vS):
    W = lambda nm: I[f"l{L}_{nm}"]
    with ExitStack() as st:
        r_ps = psres()
        ht = [sbuf(st, "aht%d" % i, [128, 8, 512], F32) for i in range(2)]
        s_ht = [cx.slot("aht%d" % i) for i in range(2)]
        u = [sbuf(st, "au%d" % i, [128, 8, 512], BF16) for i in range(2)]
        r_u = [cx.res("au%d" % i) for i in range(2)]
        sq = sbuf(st, "asq", [128, 8, 512], BF16); r_sq = cx.res("sq")
        un = sbuf(st, "aun", [128, 8, 512], F32); r_un = cx.res("un")
        rstd = sbuf(st, "arstd", [128, 512], F32); r_rstd = cx.res("rstd")
        NR = 128 if kind < 2 else 96
        Ct = sbuf(st, "Ct", [NR, S], F32); St_ = sbuf(st, "St", [NR, S], F32); s_tab = cx.slot("tab")
        perm = sbuf(st, "perm", [NR, NR], BF16); s_perm = cx.slot("perm")
        sfx = "64" if kind < 2 else "96"
        cx.dma("sp", s_tab, [], [s_tab], Ct[:, :], I["k_c" + sfx][:, :])
        cx.dma("sp", s_tab, [], [s_tab], St_[:, :], I["k_s" + sfx][:, :])
        cx.dma("pool", s_perm, [], [s_perm], perm[:, :], I["k_p" + sfx][:, :])
        ob = [sbuf(st, "aob%d" % i, [128, 512], BF16) for i in range(4)]
        s_ob = [cx.slot("aob%d" % i) for i in range(4)]
        xb = [sbuf(st, "axb%d" % i, [128, 512], BF16) for i in range(2)]
        r_xb = [cx.res("axb%d" % i) for i in range(2)]
        t1 = [sbuf(st, "at1_%d" % i, [128, 512], F32) for i in range(2)]
        r_t1 = [cx.res("at1_%d" % i) for i in range(2)]
        t2 = [sbuf(st, "at2_%d" % i, [128, 512], F32) for i in range(2)]
        r_t2 = [cx.res("at2_%d" % i) for i in range(2)]
        vb = [sbuf(st, "avb%d" % i, [128, 1024], BF16) for i in range(2)]
        s_vb = [cx.slot("avb%d" % i) for i in range(2)]
        cnt = {"p": 0, "ob": 0, "v": 0, "vb": 0, "ev": 0}
        pend = []

        def emit_fm(nk, lhsT_fn, rhs_fn, M, T, rope, t0, deps, dst, r0=0, r1=None):
            r1 = M if r1 is None else r1
            i = cnt["p"]; cnt["p"] += 1
            pb = (1, 2, 7)[i % 3]
            for k in range(nk):
                cx.op("pe", deps, [r_ps[pb]],
                      lambda h, k=k: h.matmul(PS[pb][0:M, 0:T], lhsT=lhsT_fn(k), rhs=rhs_fn(k), start=(k == 0),
                                              stop=(k == nk - 1)))
            flush_pend()
            pend.append(lambda: fm_tail(i, pb, M, T, rope, t0, dst, r0, r1))

        def flush_pend():
            while pend:
                pend.pop(0)()

        def fm_tail(i, pb, M, T, rope, t0, dst, r0, r1):
            oi = cnt["ob"] % 4; cnt["ob"] += 1
            _fc = os.environ.get("FM_CUT", "")
            if _fc == "mm":
                return
            if not rope:
                if cnt["ev"] % 2 == 0:
                    cx.op("act", [r_ps[pb]], [s_ob[oi]], lambda h: h.copy(out=ob[oi][0:M, 0:T], in_=PS[pb][0:M, 0:T]))
                else:
                    cx.op("dve", [r_ps[pb]], [s_ob[oi]],
                          lambda h: h.tensor_copy(out=ob[oi][0:M, 0:T], in_=PS[pb][0:M, 0:T]))
                cnt["ev"] += 1
            else:
                xi = i % 2
                pb2 = 3 + (i % 2)
                cx.op("act", [r_ps[pb]], [r_xb[xi]], lambda h: h.copy(out=xb[xi][0:M, 0:T], in_=PS[pb][0:M, 0:T]))
                cx.op("pe", [r_xb[xi], s_perm], [r_ps[pb2]],
                      lambda h: h.matmul(PS[pb2][0:M, 0:T], lhsT=perm[0:M, 0:M], rhs=xb[xi][0:M, 0:T], start=True,
                                         stop=True))
                _rc = os.environ.get("ROPE_CUT", "")
                if _rc == "mm":
                    return
                cx.op("dve", [r_ps[pb], s_tab, r_xb[xi]], [r_t1[xi]],
                      lambda h: h.tensor_tensor(out=t1[xi][0:M, 0:T], in0=PS[pb][0:M, 0:T], in1=Ct[0:M, t0:t0 + T],
                                                op=ALU.mult))
                if _rc == "t1":
                    return
                cx.op("dve", [r_ps[pb2], s_tab], [r_t2[xi]],
                      lambda h: h.tensor_tensor(out=t2[xi][0:M, 0:T], in0=PS[pb2][0:M, 0:T], in1=St_[0:M, t0:t0 + T],
                                                op=ALU.mult))
                if _rc == "t2":
                    return
                cx.op(os.environ.get("ROPE_ADD", "pool"), [r_t1[xi], r_t2[xi]], [s_ob[oi]],
                      lambda h: h.tensor_tensor(out=ob[oi][0:M, 0:T], in0=t1[xi][0:M, 0:T], in1=t2[xi][0:M, 0:T],
                                                op=ALU.add))
            if _fc == "ev":
                return
            if _fc == "vs":
                dst = vS[t0:t0 + 128, 0:T]
            if _fc == "pool":
                cx.dma("pool", s_ob[oi], [s_ob[oi]], [], dst, ob[oi][r0:r1, 0:T])
                return
            cx.dma("sp", s_ob[oi], [s_ob[oi]], [], dst, ob[oi][r0:r1, 0:T])

        def emit_v(nk, lhsT_fn, rhs_fn, ncol, groups, T, t0, deps):
            for s_ in range(T // 128):
                bi = cnt["vb"] % 2; cnt["vb"] += 1
                for g in range(groups):
                    pb = 5 + (cnt["v"] % 2); cnt["v"] += 1
                    for k in range(nk):
                        cx.op("pe", deps, [r_ps[pb]],
                              lambda h, k=k, pb=pb, g=g: h.matmul(PS[pb][:, 0:ncol], lhsT=lhsT_fn(k, s_),
                                                                  rhs=rhs_fn(k, g), start=(k == 0), stop=(k == nk - 1)))
                    flush_pend()
                    if g % 2 == 0:
                        cx.op("act", [r_ps[pb]], [s_vb[bi]],
                              lambda h, pb=pb, g=g: h.copy(out=vb[bi][:, g * ncol:(g + 1) * ncol], in_=PS[pb][:, 0:ncol]))
                    else:
                        cx.op("dve", [r_ps[pb]], [s_vb[bi]],
                              lambda h, pb=pb, g=g: h.tensor_copy(out=vb[bi][:, g * ncol:(g + 1) * ncol],
                                                                  in_=PS[pb][:, 0:ncol]))
                nv = ncol * groups
                cx.dma("sp", s_vb[bi], [s_vb[bi]], [], vS[t0 + s_ * 128:t0 + (s_ + 1) * 128, 0:nv], vb[bi][:, 0:nv])

        if kind == 0:
            Wt = sbuf(st, "awt", [128, 8, 1792], BF16); s_w = cx.slot("awt")
            load_w(s_w, Wt[:, :, 0:1024], W("w_qkv")[:, 0:1024], 8)
            for g in range(4):
                for hf in range(2):
                    cx.dma("pool", s_w, [], [s_w], Wt[:, :, 1024 + g * 128 + hf * 64:1024 + g * 128 + hf * 64 + 64],
                           W("w_qkv")[:, 1024 + g * 64:1024 + (g + 1) * 64].rearrange("(k p) f -> p k f", p=128))
            cx.dma("pool", s_w, [], [s_w], Wt[:, :, 1536:1792],
                   W("w_qkv")[:, 1280:1536].rearrange("(k p) f -> p k f", p=128))
            NQ, KOFF, NKC, VOFF, NVC, VG = 8, 1024, 4, 1536, 256, 1
        elif kind == 1:
            Wt = sbuf(st, "awt", [128, 8, 3072], BF16); s_w = cx.slot("awt")
            load_w(s_w, Wt, W("w_qkv"), 8)
            NQ, KOFF, NKC, VOFF, NVC, VG = 8, 1024, 8, 2048, 512, 2
        else:
            Wt = sbuf(st, "awt", [128, 8, 672], BF16); s_w = cx.slot("awt")
            load_w(s_w, Wt, W("w_in"), 8)
            Wq = sbuf(st, "awq", [128, 3, 1536], BF16); s_wq = cx.slot("awq")
            load_w(s_wq, Wq, W("w_uq"), 3)
            Wkv = sbuf(st, "awkv", [128, 2, 2, 1024], BF16); s_wkv = cx.slot("awkv")
            for k in range(2):
                for a in range(2):
                    cx.dma("pool", s_wkv, [], [s_wkv], Wkv[:, k, a, :].rearrange("p (h e) -> p h e", e=64),
                           W("w_ukv")[k * 128:(k + 1) * 128, :].rearrange("p (h e) -> p h e", e=128)[:, :, a * 64:(a + 1) * 64])
            qn = sbuf(st, "aqn", [128, 3], F32); kvn = sbuf(st, "akvn", [128, 2], F32); s_nrm = cx.slot("anrm")
            cx.dma("sp", s_nrm, [], [s_nrm], qn[:, :], W("qn")[:, :])
            cx.dma("sp", s_nrm, [], [s_nrm], kvn[:, :], W("kvn")[:, :])
            cqf = sbuf(st, "cqf", [128, 5, 512], F32); r_cqf = cx.res("cqf")
            cqn = sbuf(st, "cqn", [128, 5, 512], BF16); r_cqn = cx.res("cqn")
            rs2 = sbuf(st, "ars2", [128, 512], F32); r_rs2 = cx.res("rs2")

        def load_ht(ti):
            t0, T, strm = TILES[ti]
            b = ti % 2
            cx.dma("sp", s_ht[b], [], [s_ht[b]], ht[b][:, :, 0:T], hT[:, t0:t0 + T].rearrange("(c p) t -> p c t", p=128))

        import os
        _cut = os.environ.get("A_CUT", "")
        if _cut:
            TILES = TILES[:int(_cut[0])]
        def do_norm(ti):
            t0_, T_, strm_ = TILES[ti]
            b_ = ti % 2
            norm_front(s_ht[b_], ht[b_], T_, MV[:, 0, strm_, :], MV[:, 1, strm_, :], u[b_], r_u[b_], sq, r_sq, un, r_un,
                       rstd, r_rstd, 0, r_ps)

        load_ht(0)
        if len(TILES) > 1:
            load_ht(1)
        do_norm(0)
        for ti, (t0, T, strm) in enumerate(TILES):
            b = ti % 2
            if ti + 1 < len(TILES):
                do_norm(ti + 1)
            if ti + 2 < len(TILES):
                load_ht(ti + 2)
            lat = (strm == 0) and not os.environ.get("NO_ROPE")
            need_q = lat or (not last)
            ub = u[b]
            if _cut and "n" in _cut:
                continue
            if kind < 2:
                if need_q and "q" not in _cut:
                    for c in range(NQ):
                        emit_fm(8, lambda k, c=c: Wt[:, k, c * 128:(c + 1) * 128], lambda k: ub[:, k, 0:T], 128, T, lat,
                                t0, [s_w, r_u[b]], qS[c * 128:(c + 1) * 128, t0:t0 + T])
                for c in range(NKC if "k" not in _cut else 0):
                    _ko = 0 if os.environ.get("K_W") else KOFF
                    _kd = qS if os.environ.get("K_D") else kS
                    emit_fm(8, lambda k, c=c: Wt[:, k, _ko + c * 128:_ko + (c + 1) * 128], lambda k: ub[:, k, 0:T], 128,
                            T, lat, t0, [s_w, r_u[b]], _kd[c * 128:(c + 1) * 128, t0:t0 + T])
                if "v" not in _cut:
                  emit_v(8, lambda k, s_: ub[:, k, s_ * 128:(s_ + 1) * 128],
                       lambda k, g: Wt[:, k, VOFF + g * NVC:VOFF + (g + 1) * NVC], NVC, VG, T, t0, [s_w, r_u[b]])
            else:
                for c in range(5):
                    i = cnt["p"]; cnt["p"] += 1
                    pb = (1, 2, 7)[i % 3]
                    for k in range(8):
                        cx.op("pe", [s_w, r_u[b]], [r_ps[pb]],
                              lambda h, k=k, c=c, pb=pb: h.matmul(PS[pb][:, 0:T], lhsT=Wt[:, k, c * 128:(c + 1) * 128],
                                                                  rhs=ub[:, k, 0:T], start=(k == 0), stop=(k == 7)))
                    if c % 2 == 0:
                        cx.op("act", [r_ps[pb]], [r_cqf], lambda h, c=c, pb=pb: h.copy(out=cqf[:, c, 0:T], in_=PS[pb][:, 0:T]))
                    else:
                        cx.op("dve", [r_ps[pb]], [r_cqf],
                              lambda h, c=c, pb=pb: h.tensor_copy(out=cqf[:, c, 0:T], in_=PS[pb][:, 0:T]))
                emit_fm(8, lambda k: Wt[:, k, 576:672], lambda k: ub[:, k, 0:T], 96, T, lat, t0, [s_w, r_u[b]],
                        krS[:, t0:t0 + T], r0=64, r1=96)
                flush_pend()
                for (c0, ncn, nfeat, gv) in ((0, 3, 384, qn), (3, 2, 256, kvn)):
                    cx.op("act", [r_cqf], [r_sq],
                          lambda h, c0=c0, ncn=ncn: h.activation(out=sq[:, 0:ncn, 0:T], in_=cqf[:, c0:c0 + ncn, 0:T],
                                                                 func=AF.Square))
                    for c in range(ncn):
                        cx.op("pe", [r_sq], [r_ps[0]],
                              lambda h, c=c, ncn=ncn: h.matmul(PS[0][:, 0:T], lhsT=ones_bf[:, :], rhs=sq[:, c, 0:T],
                                                               start=(c == 0), stop=(c == ncn - 1)))
                    cx.op("act", [r_ps[0]], [r_rs2],
                          lambda h, nfeat=nfeat: h.activation(out=rs2[:, 0:T], in_=PS[0][:, 0:T], func=AF.Ln,
                                                              bias=eps_t[:, 0:1], scale=1.0 / nfeat))
                    cx.op("act", [r_rs2], [r_rs2],
                          lambda h: h.activation(out=rs2[:, 0:T], in_=rs2[:, 0:T], func=AF.Exp, scale=-0.5))
                    for c in range(ncn):
                        cx.op("dve", [r_cqf, r_rs2, s_nrm], [r_cqn],
                              lambda h, c=c, c0=c0, gv=gv: h.scalar_tensor_tensor(
                                  out=cqn[:, c0 + c, 0:T], in0=cqf[:, c0 + c, 0:T], scalar=gv[:, c:c + 1],
                                  in1=rs2[:, 0:T], op0=ALU.mult, op1=ALU.mult))
                if need_q:
                    for hq in range(16):
                        emit_fm(3, lambda k, hq=hq: Wq[:, k, hq * 96:(hq + 1) * 96], lambda k: cqn[:, k, 0:T], 96, T, lat,
                                t0, [s_wq, r_cqn], qS[hq * 128:hq * 128 + 96, t0:t0 + T])
                flush_pend()
                for hp in range(8):
                    i = cnt["p"]; cnt["p"] += 1
                    pb = (1, 2, 7)[i % 3]
                    for k in range(2):
                        cx.op("pe", [s_wkv, r_cqn], [r_ps[pb]],
                              lambda h, k=k, hp=hp, pb=pb: h.matmul(
                                  PS[pb][:, 0:T],
                                  lhsT=Wkv[:, k, 0, hp * 128:(hp + 1) * 128],
                                  rhs=cqn[:, 3 + k, 0:T], start=(k == 0), stop=(k == 1)))
                    oi = cnt["ob"] % 4; cnt["ob"] += 1
                    cx.op("act", [r_ps[pb]], [s_ob[oi]], lambda h, pb=pb, oi=oi: h.copy(out=ob[oi][:, 0:T], in_=PS[pb][:, 0:T]))
                    for a in range(2):
                        cx.dma("sp", s_ob[oi], [s_ob[oi]], [], kS[(2 * hp + a) * 128:(2 * hp + a) * 128 + 64, t0:t0 + T],
                               ob[oi][a * 64:(a + 1) * 64, 0:T])
                emit_v(2, lambda k, s_: cqn[:, 3 + k, s_ * 128:(s_ + 1) * 128],
                       lambda k, g: Wkv[:, k, 1, g * 512:(g + 1) * 512],
                       512, 2, T, t0, [s_wkv, r_cqn])
            flush_pend()
        flush_pend()
        cx.end_phase()


def phase_B(nc, cx, I, L, kind, last, sbuf, psres, PS, ones_bf, eps_t, qS, kS, krS, vS, oS, bg_make=None, bg_every=9):
    W = lambda nm: I[f"l{L}_{nm}"]
    NKT = ST // 128
    with ExitStack() as st:
        r_ps = psres()
        bg = bg_make(st, r_ps) if (bg_make is not None and kind != 1) else []
        uctr = [0]

        def tick():
            uctr[0] += 1
            if bg and uctr[0] % bg_every == 0:
                bg.pop(0)()
        r_c = cx.res("bconst")
        Q = [sbuf(st, "bq%d" % i, [128, ST], BF16) for i in range(2)]; s_Q = [cx.slot("bq%d" % i) for i in range(2)]
        K = [sbuf(st, "bk%d" % i, [128, ST], BF16) for i in range(2)]; s_K = [cx.slot("bk%d" % i) for i in range(2)]
        V = [sbuf(st, "bv%d" % i, [128, NKT, 128], BF16) for i in range(2)]; s_V = [cx.slot("bv%d" % i) for i in range(2)]
        NP = 4
        P = [sbuf(st, "bp%d" % i, [128, 512], BF16) for i in range(NP)]; r_P = [cx.res("bp%d" % i) for i in range(NP)]
        tmp = [sbuf(st, "bt%d" % i, [128, 512], F32) for i in range(6)]; r_tmp = [cx.res("bt%d" % i) for i in range(6)]
        obf = [sbuf(st, "bo%d" % i, [128, 512], BF16) for i in range(2)]; s_obf = [cx.slot("bo%d" % i) for i in range(2)]
        sqb = sbuf(st, "bsq", [128, 512], BF16); r_sqb = cx.res("bsq")
        ring = {"s": 0, "p": 0, "o": 0, "e": 0}
        pending = []
        pending_mid = []
        if kind == 0:
            mask = sbuf(st, "bmask", [128, 384], BF16); s_mask = cx.slot("bmask")
            cx.dma("pool", s_mask, [], [s_mask], mask[:, :], I["k_mask"][:, :])
            sink = sbuf(st, "bsink", [128, 16], F32); s_sink = cx.slot("bsink")
            cx.dma("sp", s_sink, [], [s_sink], sink[:, :], W("sink")[:, :])
            esink = sbuf(st, "besink", [128, 16], F32)
            cx.op("act", [s_sink], [r_c], lambda h: h.activation(out=esink[:, :], in_=sink[:, :], func=AF.Exp))
        if kind == 1:
            lam_init = 0.8 - 0.6 * math.exp(-0.3 * L)
            lam = sbuf(st, "blam", [128, 256], F32); s_lam = cx.slot("blam")
            subl = sbuf(st, "bsubl", [128, 1], F32)
            cx.dma("sp", s_lam, [], [s_lam], lam[:, :], W("lam")[:, :])
            cx.dma("sp", s_lam, [], [s_lam], subl[:, :], W("subln")[:, :])
            lt = sbuf(st, "blt", [128, 128], F32); ls = sbuf(st, "bls", [128, 4], F32)
            nlam = sbuf(st, "bnlam", [128, 1], F32); subs = sbuf(st, "bsubs", [128, 1], F32)
            r_l = cx.res("lamtmp")
            cx.op("dve", [s_lam], [r_l], lambda h: h.tensor_tensor(out=lt[:, 0:64], in0=lam[:, 0:64], in1=lam[:, 64:128], op=ALU.mult))
            cx.op("dve", [s_lam], [r_l], lambda h: h.tensor_tensor(out=lt[:, 64:128], in0=lam[:, 128:192], in1=lam[:, 192:256], op=ALU.mult))
            cx.op("dve", [r_l], [r_l], lambda h: h.reduce_sum(out=ls[:, 0:1], in_=lt[:, 0:64], axis=mybir.AxisListType.X))
            cx.op("dve", [r_l], [r_l], lambda h: h.reduce_sum(out=ls[:, 1:2], in_=lt[:, 64:128], axis=mybir.AxisListType.X))
            cx.op("act", [r_l], [r_l], lambda h: h.activation(out=ls[:, 2:4], in_=ls[:, 0:2], func=AF.Exp))
            cx.op("dve", [r_l], [r_c], lambda h: h.scalar_tensor_tensor(out=nlam[:, :], in0=ls[:, 3:4], scalar=-lam_init,
                                                                        in1=ls[:, 2:3], op0=ALU.add, op1=ALU.subtract))
            cx.op("dve", [s_lam], [r_c], lambda h: h.tensor_scalar(out=subs[:, :], in0=subl[:, :], scalar1=1.0 - lam_init,
                                                                   scalar2=None, op0=ALU.mult))
        if kind != 1:
            for i in range(2):
                cx.op("dve", [], [s_V[i]], lambda h, i=i: h.memset(V[i][:, :, 64:128], 1.0))

        def run_unit(steps, epilogue, gsize=1):
            groups = [steps[i:i + gsize] for i in range(0, len(steps), gsize)]
            ng = len(groups)
            nring = 4 // gsize if gsize > 1 else 4
            GLA = max(1, nring - 1) if gsize == 1 else 1
            GLA = int(os.environ.get("B_GLA", GLA))

            def emit_qk(g):
                for s in groups[g]:
                    sbk = ring["s"] % 4; ring["s"] += 1
                    s["sb"] = sbk
                    N = s["N"]
                    cx.op("pe", s["rd"], [r_ps[sbk]],
                          lambda h, s=s, sbk=sbk, N=N: h.matmul(PS[sbk][:, 0:N], lhsT=s["k"], rhs=s["q"], start=True,
                                                                stop=True))

            for g in range(min(GLA, ng)):
                emit_qk(g)
            for g in range(ng):
                if g + GLA < ng:
                    emit_qk(g + GLA)
                if g == min(2, ng - 1):
                    while pending_mid:
                        pending_mid.pop(0)()
                for s in groups[g]:
                    N = s["N"]
                    pi = ring["p"] % NP; ring["p"] += 1
                    s["pi"] = pi
                    sbk = s["sb"]
                    cx.op("act", [r_ps[sbk]], [r_P[pi]],
                          lambda h, s=s, N=N, pi=pi, sbk=sbk: h.activation(out=P[pi][:, 0:N], in_=PS[sbk][:, 0:N],
                                                                           func=AF.Exp, scale=s["scale"]))
                    if s.get("mask") is not None:
                        cx.op("dve", [r_P[pi], s_mask], [r_P[pi]],
                              lambda h, s=s, N=N, pi=pi: h.tensor_tensor(out=P[pi][:, 0:N], in0=P[pi][:, 0:N],
                                                                         in1=s["mask"], op=ALU.mult))
                for s in groups[g]:
                    N = s["N"]
                    pi = s["pi"]
                    for (obk, c0, lhsT, start, stop, rd) in s["pv"]:
                        cx.op("pe", [r_P[pi]] + rd, [r_ps[obk]],
                              lambda h, obk=obk, c0=c0, lhsT=lhsT, start=start, stop=stop, N=N, pi=pi: h.matmul(
                                  PS[obk][:, c0:c0 + N], lhsT=lhsT, rhs=P[pi][:, 0:N], start=start, stop=stop,
                                  skip_group_check=True))
            while pending:
                pending.pop(0)()
            epilogue()

        def epi_aug(obk, N, sink_col, dst):
            e = ring["e"] % 2; ring["e"] += 1
            epi_aug_body(obk, N, sink_col, dst, e)

        def epi_aug_body(obk, N, sink_col, dst, e):
            ta, tb = tmp[2 * e], tmp[2 * e + 1]
            ra, rb = r_tmp[2 * e], r_tmp[2 * e + 1]
            if sink_col is not None:
                cx.op("act", [r_ps[obk], r_c], [ra],
                      lambda h: h.activation(out=ta[64:128, 0:N], in_=PS[obk][64:128, 0:N], func=AF.Ln,
                                             bias=esink[64:128, sink_col:sink_col + 1], scale=1.0))
                cx.op("act", [ra], [rb],
                      lambda h: h.activation(out=tb[0:64, 0:N], in_=ta[64:128, 0:N], func=AF.Exp, scale=-1.0))
            else:
                cx.op("dve", [r_ps[obk]], [ra], lambda h: h.reciprocal(out=ta[64:128, 0:N], in_=PS[obk][64:128, 0:N]))
                cx.op("dve", [ra], [rb], lambda h: h.tensor_copy(out=tb[0:64, 0:N], in_=ta[64:128, 0:N]))
            cx.op("dve", [r_ps[obk], rb], [s_obf[e]],
                  lambda h: h.tensor_tensor(out=obf[e][0:64, 0:N], in0=PS[obk][0:64, 0:N], in1=tb[0:64, 0:N], op=ALU.mult))
            cx.dma("sp", s_obf[e], [s_obf[e]], [], dst, obf[e][0:64, 0:N])

        def epi_diff(N, dst):
            e = ring["e"] % 2; ring["e"] += 1
            t = tmp; r = r_tmp
            cx.op("dve", [r_ps[6]], [r[0]], lambda h: h.tensor_copy(out=t[0][:, 0:N], in_=PS[6][:, 0:N]))
            cx.op("dve", [r_ps[7]], [r[1]], lambda h: h.tensor_copy(out=t[1][:, 0:N], in_=PS[7][:, 0:N]))
            cx.op("dve", [r_ps[4]], [r[2]], lambda h: h.tensor_copy(out=t[2][:, 0:N], in_=PS[4][:, 0:N]))
            cx.op("dve", [r_ps[5]], [r[3]], lambda h: h.tensor_copy(out=t[3][:, 0:N], in_=PS[5][:, 0:N]))
            cx.op("dve", [r[0]], [r[0]], lambda h: h.reciprocal(out=t[0][:, 0:N], in_=t[0][:, 0:N]))
            cx.op("dve", [r[1]], [r[1]], lambda h: h.reciprocal(out=t[1][:, 0:N], in_=t[1][:, 0:N]))
            cx.op("dve", [r[2], r[0]], [r[2]],
                  lambda h: h.tensor_tensor(out=t[2][:, 0:N], in0=t[2][:, 0:N], in1=t[0][:, 0:N], op=ALU.mult))
            cx.op("dve", [r[3], r[1]], [r[3]],
                  lambda h: h.tensor_tensor(out=t[3][:, 0:N], in0=t[3][:, 0:N], in1=t[1][:, 0:N], op=ALU.mult))
            cx.op("dve", [r[2], r[3], r_c], [r[4]],
                  lambda h: h.scalar_tensor_tensor(out=t[4][:, 0:N], in0=t[3][:, 0:N], scalar=nlam[:, 0:1],
                                                   in1=t[2][:, 0:N], op0=ALU.mult, op1=ALU.add))
            cx.op("pool", [r[4]], [r_sqb],
                  lambda h: h.tensor_tensor(out=sqb[:, 0:N], in0=t[4][:, 0:N], in1=t[4][:, 0:N], op=ALU.mult))
            pending.append(lambda: epi_diff2(N, dst, e))

        def epi_diff2(N, dst, e):
            t = tmp; r = r_tmp
            sbk = ring["s"] % 4; ring["s"] += 1
            cx.op("pe", [r_sqb], [r_ps[sbk]],
                  lambda h: h.matmul(PS[sbk][:, 0:N], lhsT=ones_bf[:, :], rhs=sqb[:, 0:N], start=True, stop=True))
            cx.op("act", [r_ps[sbk]], [r[5]],
                  lambda h: h.activation(out=t[5][:, 0:N], in_=PS[sbk][:, 0:N], func=AF.Ln, bias=eps_t[:, 0:1],
                                         scale=1.0 / 128))
            cx.op("act", [r[5]], [r[5]],
                  lambda h: h.activation(out=t[5][:, 0:N], in_=t[5][:, 0:N], func=AF.Exp, scale=-0.5))
            cx.op("dve", [r[4], r[5], r_c], [s_obf[e]],
                  lambda h: h.scalar_tensor_tensor(out=obf[e][:, 0:N], in0=t[4][:, 0:N], scalar=subs[:, 0:1],
                                                   in1=t[5][:, 0:N], op0=ALU.mult, op1=ALU.mult))
            cx.dma("sp", s_obf[e], [s_obf[e]], [], dst, obf[e][:, 0:N])

        qtiles = [(t0, 512) for t0 in range(0, S, 512)] + ([] if last else [(S, CL)])

        if kind == 0:
            sc = 64 ** -0.5
            nq = 0
            def load_kv0(g):
                kb = g % 2
                cx.dma("sp", s_K[kb], [], [s_K[kb]], K[kb][:, :], kS[g * 128:(g + 1) * 128, :])
                cx.dma("sp", s_V[kb], [], [s_V[kb]], V[kb][:, :, 0:64],
                       vS[:, g * 64:(g + 1) * 64].rearrange("(j p) e -> p j e", p=128))

            def load_q0(c):
                cx.dma("sp", s_Q[c % 2], [], [s_Q[c % 2]], Q[c % 2][:, :], qS[c * 128:(c + 1) * 128, :])

            load_kv0(0)
            load_q0(0)
            for g in range(4):
                kb = g % 2
                if g + 1 < 4:
                    load_kv0(g + 1)
                for cc in range(2):
                    c = 2 * g + cc
                    qb = nq % 2; nq += 1
                    if c + 1 < 8:
                        load_q0(c + 1)
                    for hf in range(2):
                        hd = 2 * c + hf
                        psl = slice(64 * hf, 64 * hf + 64)
                        for (t0, N0) in qtiles:
                            obk = 4 + (ring["o"] % 2); ring["o"] += 1
                            rd = [s_K[kb], s_Q[qb]]
                            steps = []
                            for i in range(2):
                                steps.append(dict(k=K[kb][psl, S + 128 * i:S + 128 * (i + 1)], q=Q[qb][psl, t0:t0 + N0],
                                                  N=N0, scale=sc, rd=rd,
                                                  pv=[(obk, 0, V[kb][:, 32 + i, :], i == 0, False, [s_V[kb]])]))
                            if t0 < S:
                                for j in range(6):
                                    kp0 = t0 - 128 + 128 * j
                                    if kp0 < 0 or kp0 >= S:
                                        continue
                                    bmin, bmax = max(0, j - 2), min(3, j)
                                    N = 128 * (bmax - bmin + 1)
                                    steps.append(dict(k=K[kb][psl, kp0:kp0 + 128],
                                                      q=Q[qb][psl, t0 + 128 * bmin:t0 + 128 * bmin + N], N=N, scale=sc, rd=rd,
                                                      mask=mask[:, 128 * (bmin - j + 2):128 * (bmin - j + 2) + N],
                                                      pv=[(obk, 128 * bmin, V[kb][:, kp0 // 128, :], False, False, [s_V[kb]])]))
                            lp = steps[-1]["pv"][0]
                            steps[-1]["pv"][0] = (lp[0], lp[1], lp[2], lp[3], True, lp[5])
                            run_unit(steps, lambda obk=obk, N0=N0, hd=hd, t0=t0: epi_aug(
                                obk, N0, hd, oS[hd * 64:(hd + 1) * 64, t0:t0 + N0]))
                            tick()
        elif kind == 2:
            sc = 96 ** -0.5
            def load_head2(hd):
                b = hd % 2
                cx.dma("sp", s_K[b], [], [s_K[b]], K[b][0:64, :], kS[hd * 128:hd * 128 + 64, :])
                cx.dma("sp", s_K[b], [], [s_K[b]], K[b][64:96, :], krS[:, :])
                cx.dma("sp", s_Q[b], [], [s_Q[b]], Q[b][0:96, :], qS[hd * 128:hd * 128 + 96, :])
                cx.dma("sp", s_V[b], [], [s_V[b]], V[b][:, :, 0:64],
                       vS[:, hd * 64:(hd + 1) * 64].rearrange("(j p) e -> p j e", p=128))

            load_head2(0)
            for hd in range(16):
                b = hd % 2
                if hd + 1 < 16:
                    load_head2(hd + 1)
                for (t0, N0) in qtiles:
                    obk = 4 + (ring["o"] % 2); ring["o"] += 1
                    rd = [s_K[b], s_Q[b]]
                    js = list(range(NKT)) if t0 < S else [32, 33]
                    steps = [dict(k=K[b][0:96, 128 * j:128 * (j + 1)], q=Q[b][0:96, t0:t0 + N0], N=N0, scale=sc, rd=rd,
                                  pv=[(obk, 0, V[b][:, j, :], j == js[0], j == js[-1], [s_V[b]])]) for j in js]
                    run_unit(steps, lambda obk=obk, N0=N0, hd=hd, t0=t0: epi_aug(
                        obk, N0, None, oS[hd * 64:(hd + 1) * 64, t0:t0 + N0]))
                    tick()
        else:
            sc = 64 ** -0.5
            def load_head1(hd):
                b = hd % 2
                cx.dma("sp", s_K[b], [], [s_K[b]], K[b][:, :], kS[hd * 128:(hd + 1) * 128, :])
                cx.dma("sp", s_Q[b], [], [s_Q[b]], Q[b][:, :], qS[hd * 128:(hd + 1) * 128, :])
                cx.dma("sp", s_V[b], [], [s_V[b]], V[b][:, :, :],
                       vS[:, hd * 128:(hd + 1) * 128].rearrange("(j p) e -> p j e", p=128))

            load_head1(0)
            for hd in range(8):
                b = hd % 2
                if hd + 1 < 8:
                    load_head1(hd + 1)
                for (t0, N0) in qtiles:
                    rd = [s_K[b], s_Q[b]]
                    js = list(range(NKT)) if t0 < S else [32, 33]
                    steps = []
                    for j in js:
                        for m in range(2):
                            psl = slice(64 * m, 64 * m + 64)
                            steps.append(dict(k=K[b][psl, 128 * j:128 * (j + 1)], q=Q[b][psl, t0:t0 + N0], N=N0, scale=sc,
                                              rd=rd, pv=[(4 + m, 0, V[b][:, j, :], j == js[0], j == js[-1], [s_V[b]]),
                                                         (6 + m, 0, ones_bf[:, :], j == js[0], j == js[-1], [])]))
                    run_unit(steps, lambda N0=N0, hd=hd, t0=t0: epi_diff(N0, oS[hd * 128:(hd + 1) * 128, t0:t0 + N0]),
                             gsize=int(os.environ.get("B_GS", 2)))
        while pending:
            pending.pop(0)()
        while pending_mid:
            pending_mid.pop(0)()
        while bg:
            bg.pop(0)()
        cx.end_phase()


def _layout_inputs(inputs, b, n_layers=4):
    f = lambda a: np.ascontiguousarray(a, dtype=np.float32)
    m = {"x": f(inputs["x"][b]), "ctx": f(inputs["ctx"][b])}
    c2 = np.stack([inputs["c"][b].reshape(8, 128).T, inputs["c_ctx"].reshape(8, 128).T], axis=-1)
    m["c2"] = f(c2)
    m.update(_consts())
    for L in range(n_layers):
        p = lambda nm: inputs[f"l{L}_{nm}"]
        m[f"l{L}_ada_w"] = f(p("ada_w"))
        m[f"l{L}_ada_b"] = f(p("ada_b").reshape(48, 128).T)
        m[f"l{L}_norms"] = f(p("norms").reshape(4, 8, 128).transpose(2, 0, 1))
        k = L % 3
        if k == 0:
            m[f"l{L}_w_qkv"] = f(p("w_qkv")); m[f"l{L}_w_o"] = f(p("w_o"))
            m[f"l{L}_sink"] = f(np.broadcast_to(p("sink")[None, :], (128, 16)))
        elif k == 1:
            m[f"l{L}_w_qkv"] = f(p("w_qkv")); m[f"l{L}_w_o"] = f(p("w_o"))
            m[f"l{L}_lam"] = f(np.broadcast_to(p("lambda").reshape(1, 256), (128, 256)))
            m[f"l{L}_subln"] = f(p("subln").reshape(128, 1))
        else:
            m[f"l{L}_w_in"] = f(p("w_in")); m[f"l{L}_w_uq"] = f(p("w_uq")); m[f"l{L}_w_ukv"] = f(p("w_ukv"))
            m[f"l{L}_w_o"] = f(p("w_o"))
            m[f"l{L}_qn"] = f(p("q_norm").reshape(3, 128).T)
            m[f"l{L}_kvn"] = f(p("kv_norm").reshape(2, 128).T)
        m[f"l{L}_mlp_w1"] = f(p("mlp_w1").reshape(8, 128, 8, 512).transpose(2, 1, 0, 3).reshape(8, 128, 8 * 512))
        m[f"l{L}_mlp_w2"] = f(p("mlp_w2").reshape(32, 128, 8, 128).transpose(2, 1, 0, 3).reshape(8, 128, 32 * 128))
    return m


def kernel(**inputs):
    inputs = {k: np.asarray(v) for k, v in inputs.items()}
    nc, _ = build(4)
    in_maps = [_layout_inputs(inputs, b) for b in range(NCORES)]
    res = run_bass_kernel_spmd(nc, in_maps, core_ids=list(range(NCORES)))
    return np.stack([np.asarray(r["out"]) for r in res.results], axis=0).astype(np.float32)
```
